# Optimizing a Trainium2 kernel written in Bass

```python
import math
import jax, jax.numpy as jnp
from jax import lax
import numpy as np

D_MODEL = 1024
BATCH = 16
SEQ = 4096
DEPTH = 4

PLE_DIM = 256
HEAD_DIM = 64
N_Q_HEADS = 8
N_KV_HEADS = 2
GQA_GROUP = N_Q_HEADS // N_KV_HEADS
WINDOW = 128
BLOCK = 128
ROPE_THETA = 500000.0
ROPE_DIM = HEAD_DIM // 4
Q_WIDTH = N_Q_HEADS * HEAD_DIM
KV_WIDTH = N_KV_HEADS * HEAD_DIM
SSM_WIDTH = D_MODEL // 4
SSM_GROUP = 16
SSM_GROUPS = SSM_WIDTH // SSM_GROUP
SSM_STATE = 64
CONV_WIDTH = D_MODEL // 4
CONV_K = 31
FFN_HIDDEN = ((-(-8 * D_MODEL // 3) + 255) // 256) * 256
N_BRANCH = 3
IN_WIDTHS = [Q_WIDTH, KV_WIDTH, KV_WIDTH, SSM_WIDTH, 2 * CONV_WIDTH, N_BRANCH * D_MODEL]
IN_WIDTH = sum(IN_WIDTHS)
SPLIT_POINTS = [int(v) for v in np.cumsum(IN_WIDTHS)[:-1]]
EPS = 1e-6
NEG_INF = -1e30

kernel_name = "hybrid_gated_swa_s5_conformer_block"


def rms_norm(t, g):
    t32 = t.astype(jnp.float32)
    out = t32 * lax.rsqrt(jnp.mean(t32 * t32, axis=-1, keepdims=True) + EPS) * g.astype(jnp.float32)
    return out.astype(t.dtype)


def layer_norm(t, g, b):
    t32 = t.astype(jnp.float32)
    mu = jnp.mean(t32, axis=-1, keepdims=True)
    var = jnp.mean(jnp.square(t32 - mu), axis=-1, keepdims=True)
    out = (t32 - mu) * lax.rsqrt(var + EPS) * g.astype(jnp.float32) + b.astype(jnp.float32)
    return out.astype(t.dtype)


def partial_rope(t, cos, sin):
    half = ROPE_DIM // 2
    t1 = t[..., :half]
    t2 = t[..., half:ROPE_DIM]
    return jnp.concatenate([t1 * cos - t2 * sin, t2 * cos + t1 * sin, t[..., ROPE_DIM:]], axis=-1)


def sliding_window_attention(q, k, v, sinks):
    b, s = q.shape[0], q.shape[1]
    nb = s // BLOCK
    qb = q.reshape(b, nb, BLOCK, N_KV_HEADS, GQA_GROUP, HEAD_DIM)

    def band(t):
        tb = t.reshape(b, nb, BLOCK, N_KV_HEADS, HEAD_DIM)
        prev = jnp.pad(tb, ((0, 0), (1, 0), (0, 0), (0, 0), (0, 0)))[:, :-1]
        return jnp.concatenate([prev, tb], axis=2)

    kb, vb = band(k), band(v)
    scores = jnp.einsum('bnqkgd,bnjkd->bnkgqj', qb, kb) * (HEAD_DIM ** -0.5)
    qi = jnp.arange(BLOCK)[:, None]
    kj = jnp.arange(2 * BLOCK)[None, :]
    dist = qi + BLOCK - kj
    in_window = (dist >= 0) & (dist < WINDOW)
    has_prev = (jnp.arange(nb)[:, None, None] > 0) | (kj[None] >= BLOCK)
    mask = in_window[None] & has_prev
    scores = jnp.where(mask[None, :, None, None], scores, NEG_INF)
    sink = sinks.astype(jnp.float32).reshape(N_KV_HEADS, GQA_GROUP)[None, None, :, :, None, None]
    m = jnp.maximum(jnp.max(scores, axis=-1, keepdims=True), sink)
    pr = jnp.exp(scores - m)
    denom = jnp.sum(pr, axis=-1, keepdims=True) + jnp.exp(sink - m)
    out = jnp.einsum('bnkgqj,bnjkd->bnqkgd', pr / denom, vb)
    return out.reshape(b, s, Q_WIDTH)


def s5_ssm(u, lam_re, lam_im, log_dt, b_re, b_im, c_re, c_im, d_skip):
    b, s, _ = u.shape
    ug = u.reshape(b, s, SSM_GROUPS, SSM_GROUP)
    lr = jnp.minimum(lam_re.astype(jnp.float32), -1e-4)
    li = lam_im.astype(jnp.float32)
    dt = jnp.exp(log_dt.astype(jnp.float32))[:, None]
    mag = jnp.exp(lr * dt)
    a_re = mag * jnp.cos(li * dt)
    a_im = mag * jnp.sin(li * dt)
    den = lr * lr + li * li
    x_re, x_im = a_re - 1.0, a_im
    f_re = (x_re * lr + x_im * li) / den
    f_im = (x_im * lr - x_re * li) / den
    br = b_re.astype(jnp.float32)
    bi = b_im.astype(jnp.float32)
    bb_re = f_re[..., None] * br - f_im[..., None] * bi
    bb_im = f_re[..., None] * bi + f_im[..., None] * br
    bu_re = jnp.einsum('bsgh,gnh->bsgn', ug, bb_re)
    bu_im = jnp.einsum('bsgh,gnh->bsgn', ug, bb_im)
    a_re_t = jnp.broadcast_to(a_re, (1, s) + a_re.shape)
    a_im_t = jnp.broadcast_to(a_im, (1, s) + a_im.shape)

    def combine(e1, e2):
        a1r, a1i, b1r, b1i = e1
        a2r, a2i, b2r, b2i = e2
        return (a2r * a1r - a2i * a1i, a2r * a1i + a2i * a1r,
                a2r * b1r - a2i * b1i + b2r, a2r * b1i + a2i * b1r + b2i)

    _, _, st_re, st_im = lax.associative_scan(combine, (a_re_t, a_im_t, bu_re, bu_im), axis=1)
    y = (jnp.einsum('bsgn,ghn->bsgh', st_re, c_re.astype(jnp.float32))
         - jnp.einsum('bsgn,ghn->bsgh', st_im, c_im.astype(jnp.float32)))
    return y.reshape(b, s, SSM_WIDTH) + d_skip.astype(jnp.float32) * u


def conformer_conv(c_in, dw_w, dw_b, ln_g, ln_b, w_pw_out):
    a, g = jnp.split(c_in, 2, axis=-1)
    u = a * jax.nn.sigmoid(g)
    u = lax.conv_general_dilated(u, dw_w[:, None, :].astype(u.dtype), window_strides=(1,),
                                 padding=[(CONV_K - 1, 0)],
                                 dimension_numbers=('NWC', 'WIO', 'NWC'),
                                 feature_group_count=CONV_WIDTH) + dw_b
    u = jax.nn.silu(layer_norm(u, ln_g, ln_b))
    return u @ w_pw_out


def setup_inputs(seed: int = 0) -> dict:
    key = jax.random.key(seed)
    ks = iter(jax.random.split(key, 40))
    f32 = jnp.float32

    def nrm(shape, scale):
        return jax.random.normal(next(ks), shape, f32) * scale

    L, D = DEPTH, D_MODEL
    x = nrm((BATCH, SEQ, D), 1.0)
    p = nrm((DEPTH, BATCH, SEQ, PLE_DIM), 1.0)
    positions = (jnp.arange(SEQ, dtype=jnp.int32)[None, :]
                 + jax.random.randint(next(ks), (BATCH, 1), 0, 1024, dtype=jnp.int32))
    n_idx = jnp.arange(SSM_STATE, dtype=f32)
    return {
        'x': x,
        'p': p,
        'positions': positions,
        'mix_norm_g': 1.0 + nrm((L, D), 0.02),
        'w_in': nrm((L, D, IN_WIDTH), D ** -0.5),
        'b_gate': nrm((L, N_BRANCH * D), 0.02),
        'attn_sinks': nrm((L, N_Q_HEADS), 0.5),
        'w_attn_out': nrm((L, Q_WIDTH, D), Q_WIDTH ** -0.5),
        'ssm_lambda_re': -0.5 + nrm((L, SSM_GROUPS, SSM_STATE), 0.01),
        'ssm_lambda_im': math.pi * n_idx + nrm((L, SSM_GROUPS, SSM_STATE), 0.01),
        'ssm_log_dt': jax.random.uniform(next(ks), (L, SSM_GROUPS), f32, math.log(1e-3), math.log(1e-1)),
        'ssm_b_re': nrm((L, SSM_GROUPS, SSM_STATE, SSM_GROUP), (2 * SSM_GROUP) ** -0.5),
        'ssm_b_im': nrm((L, SSM_GROUPS, SSM_STATE, SSM_GROUP), (2 * SSM_GROUP) ** -0.5),
        'ssm_c_re': nrm((L, SSM_GROUPS, SSM_GROUP, SSM_STATE), (2 * SSM_STATE) ** -0.5),
        'ssm_c_im': nrm((L, SSM_GROUPS, SSM_GROUP, SSM_STATE), (2 * SSM_STATE) ** -0.5),
        'ssm_d': nrm((L, SSM_WIDTH), 1.0),
        'w_ssm_glu': nrm((L, SSM_WIDTH, 2 * D), SSM_WIDTH ** -0.5),
        'b_ssm_glu': nrm((L, 2 * D), 0.02),
        'conv_dw_w': nrm((L, CONV_K, CONV_WIDTH), CONV_K ** -0.5),
        'conv_dw_b': nrm((L, CONV_WIDTH), 0.02),
        'conv_norm_g': 1.0 + nrm((L, CONV_WIDTH), 0.02),
        'conv_norm_b': nrm((L, CONV_WIDTH), 0.02),
        'w_conv_out': nrm((L, CONV_WIDTH, D), CONV_WIDTH ** -0.5),
        'w_mix_out': nrm((L, D, D), D ** -0.5),
        'ffn_norm_g': 1.0 + nrm((L, D), 0.02),
        'w_ffn_in': nrm((L, D, 2 * FFN_HIDDEN), D ** -0.5),
        'w_ffn_out': nrm((L, FFN_HIDDEN, D), FFN_HIDDEN ** -0.5),
        'w_ple_in': nrm((L, PLE_DIM, D), PLE_DIM ** -0.5),
        'ple_norm_g': 1.0 + nrm((L, D), 0.02),
        'w_ple_gate': nrm((L, D, D), D ** -0.5),
        'final_norm_g': 1.0 + nrm((D,), 0.02),
    }


def reference(x, p, positions, mix_norm_g, w_in, b_gate, attn_sinks, w_attn_out,
              ssm_lambda_re, ssm_lambda_im, ssm_log_dt, ssm_b_re, ssm_b_im, ssm_c_re, ssm_c_im,
              ssm_d, w_ssm_glu, b_ssm_glu, conv_dw_w, conv_dw_b, conv_norm_g, conv_norm_b,
              w_conv_out, w_mix_out, ffn_norm_g, w_ffn_in, w_ffn_out, w_ple_in, ple_norm_g,
              w_ple_gate, final_norm_g):
    f32 = jnp.float32
    b, s, d = x.shape
    inv_freq = ROPE_THETA ** (-jnp.arange(0, ROPE_DIM, 2, dtype=f32) / ROPE_DIM)
    ang = positions.astype(f32)[..., None] * inv_freq
    cos = jnp.cos(ang)[:, :, None, :]
    sin = jnp.sin(ang)[:, :, None, :]

    for i in range(DEPTH):
        h = rms_norm(x, mix_norm_g[i])
        z = h @ w_in[i]
        q, k, v, s_in, c_in, g_in = jnp.split(z, SPLIT_POINTS, axis=-1)

        qh = partial_rope(q.astype(f32).reshape(b, s, N_Q_HEADS, HEAD_DIM), cos, sin)
        kh = partial_rope(k.astype(f32).reshape(b, s, N_KV_HEADS, HEAD_DIM), cos, sin)
        vh = v.astype(f32).reshape(b, s, N_KV_HEADS, HEAD_DIM)
        y_attn = sliding_window_attention(qh, kh, vh, attn_sinks[i]).astype(x.dtype) @ w_attn_out[i]

        y_s = s5_ssm(s_in.astype(f32), ssm_lambda_re[i], ssm_lambda_im[i], ssm_log_dt[i],
                     ssm_b_re[i], ssm_b_im[i], ssm_c_re[i], ssm_c_im[i], ssm_d[i])
        glu_a, glu_b = jnp.split(jax.nn.gelu(y_s).astype(x.dtype) @ w_ssm_glu[i] + b_ssm_glu[i], 2, axis=-1)
        y_ssm = glu_a * jax.nn.sigmoid(glu_b)

        y_conv = conformer_conv(c_in, conv_dw_w[i], conv_dw_b[i], conv_norm_g[i], conv_norm_b[i], w_conv_out[i])

        gates = jax.nn.sigmoid(g_in + b_gate[i]).reshape(b, s, N_BRANCH, d)
        merged = gates[:, :, 0] * y_attn + gates[:, :, 1] * y_ssm + gates[:, :, 2] * y_conv
        x = x + merged @ w_mix_out[i]

        hf = rms_norm(x, ffn_norm_g[i])
        f_gate, f_up = jnp.split(hf @ w_ffn_in[i], 2, axis=-1)
        x = x + (jax.nn.silu(f_gate) * f_up) @ w_ffn_out[i]

        e = p[i] @ w_ple_in[i]
        g_ple = jax.nn.sigmoid(rms_norm(x, ple_norm_g[i]) @ w_ple_gate[i])
        x = x + g_ple * e

    return rms_norm(x, final_norm_g)
```

```python
import math
from contextlib import ExitStack

import numpy as np
import concourse.bass as bass
import concourse.mybir as mybir
from concourse.bass_utils import run_bass_kernel_spmd

F32 = mybir.dt.float32
BF16 = mybir.dt.bfloat16
I32 = mybir.dt.int32
AF = mybir.ActivationFunctionType
ALU = mybir.AluOpType
AX = mybir.AxisListType

D = 1024
T = 512
DEPTH = 4
NCORE = 8
SEQ = 4096
EPS = 1e-6
TWO_PI = 2.0 * math.pi
MAGIC = 12582912.0
NEG = -30000.0
NBUF = 3
WSLOT = 4096


class Sched:
    ENG = ("pe", "act", "dve", "pool", "sp")

    def __init__(self, nc):
        self.nc = nc
        self.ops = {e: [] for e in self.ENG}
        self.cnt = {}
        self.seen = {e: {} for e in self.ENG}
        self.lastw = {}
        self.readers = {}

    def _deps(self, engine, reads, writes):
        waits = {}

        def need(k, v):
            if k == engine and engine == "pe":
                return
            if self.seen[engine].get(k, 0) >= v:
                return
            if waits.get(k, 0) < v:
                waits[k] = v

        for r in reads:
            lw = self.lastw.get(r)
            if lw is not None:
                need(*lw)
        for w in writes:
            lw = self.lastw.get(w)
            if lw is not None:
                need(*lw)
            for k, v in self.readers.get(w, {}).items():
                need(k, v)
        for k, v in waits.items():
            self.seen[engine][k] = v
        return waits

    def _commit(self, key, val, reads, writes):
        for r in reads:
            self.readers.setdefault(r, {})[key] = val
        for w in writes:
            self.lastw[w] = (key, val)
            self.readers[w] = {}

    @staticmethod
    def _record(fn):
        calls = []

        class _P:
            def __getattr__(self, name):
                def f(*a, **k):
                    calls.append((name, a, k))
                    return self
                return f
        fn(_P())
        assert calls, "op emitted no instruction"

        def replay(eng):
            ins = None
            for name, a, k in calls:
                ins = getattr(eng, name)(*a, **k)
            return ins
        return replay

    def op(self, engine, fn, reads=(), writes=()):
        fn = self._record(fn)
        px = [r for r in reads if r.startswith("ps")]
        if px:
            writes = list(writes) + [r for r in px if r not in writes]
        waits = self._deps(engine, reads, writes)
        val = self.cnt.get(engine, 0) + 1
        self.cnt[engine] = val
        self.ops[engine].append((waits, fn, engine, 1))
        self._commit(engine, val, reads, writes)

    def dma(self, queue, fn, slot, reads=(), writes=()):
        fn = self._record(fn)
        key = "dma:" + slot
        waits = self._deps(queue, reads, writes)
        val = self.cnt.get(key, 0) + 16
        self.cnt[key] = val
        self.ops[queue].append((waits, fn, key, 16))
        self._commit(key, val, reads, writes)

    def final_wait(self, queue, slots):
        waits = {}
        for s in slots:
            key = "dma:" + s
            if key in self.cnt:
                waits[key] = self.cnt[key]
        self.ops[queue].append((waits, None, None, 0))

    def emit(self, stack):
        nc = self.nc
        sems = {}
        for i, k in enumerate(list(self.cnt)):
            sems[k] = stack.enter_context(nc.semaphore("s%d_%s" % (i, k.replace(":", "_"))))
        block = stack.enter_context(nc.Block())
        ops = self.ops

        def run(eng, lst):
            for waits, fn, key, amt in lst:
                for k, v in waits.items():
                    eng.wait_ge(sems[k], v)
                if fn is not None:
                    fn(eng).then_inc(sems[key], amt)

        @block.tensor
        def _(e):
            run(e, ops["pe"])

        @block.scalar
        def _(e):
            run(e, ops["act"])

        @block.vector
        def _(e):
            run(e, ops["dve"])

        @block.gpsimd
        def _(e):
            run(e, ops["pool"])

        @block.sync
        def _(e):
            run(e, ops["sp"])


def _pack(w):
    kcp = w.shape[0] // 128
    return np.ascontiguousarray(w.reshape(kcp, 128, -1).transpose(1, 0, 2)).reshape(-1)


def _partner(r):
    if r < 8:
        return r + 8
    if r < 16:
        return r - 8
    return r


_PM = np.array([_partner(r) for r in range(64)])

PIECES = {}
LW = 0


def _mk_pieces():
    global LW
    off = 0

    def add(name, kcp, ncols, n):
        nonlocal off
        lst = []
        for _ in range(n):
            lst.append((off, kcp, ncols))
            off += 128 * kcp * ncols
        PIECES[name] = lst

    add("win", 8, 512, 11)
    add("wao", 4, 512, 2)
    add("wco", 2, 512, 2)
    add("bpad", 16, 128, 1)
    add("wglu", 2, 512, 4)
    add("wmo", 8, 512, 2)
    add("wfi", 8, 512, 11)
    PIECES["wfo"] = []
    for ng in range(2):
        for kcp in (8, 8, 6):
            PIECES["wfo"].append((off, kcp, 512))
            off += 128 * kcp * 512
    add("wpi", 2, 512, 2)
    add("wpg", 8, 512, 2)
    LW = off


_mk_pieces()

SMALL_FIELDS = [("mixg", 8), ("ffng", 8), ("pleg", 8), ("bgate", 24), ("bglu", 16), ("convw", 62),
                ("convb", 2), ("lng", 2), ("lnb", 2), ("ssmd", 2), ("sinks", 8), ("lamre", 8),
                ("lamim", 8), ("logdt", 8)]
SOFF = {}
_o = 0
for _n, _w in SMALL_FIELDS:
    SOFF[_n] = (_o, _w)
    _o += _w
LS = _o
G_FINALG = DEPTH * LS
G_INVF = G_FINALG + 8
G_SGN = G_INVF + 1
NS = G_SGN + 1
C_ID, C_MN, C_MF, C_TV, NCST = 0, 128, 384, 640, 768


def _chunked(v):
    return np.ascontiguousarray(v.reshape(-1, 128).T)


def host_prep(inp):
    f = np.float32
    wf = np.zeros((DEPTH, LW), f)
    craw = np.zeros((DEPTH, 128, 2, 8, 128), f)
    small = np.zeros((128, NS), f)
    for l in range(DEPTH):
        w_in = np.asarray(inp["w_in"][l], f)
        cq = np.arange(512)
        cqp = np.concatenate([h * 64 + _PM for h in range(8)])
        ck = np.concatenate([512 + g * 64 + np.arange(64) for g in (0, 0, 1, 1)])
        ckp = np.concatenate([512 + g * 64 + _PM for g in (0, 0, 1, 1)])
        cv = 640 + np.arange(128)
        cs = 768 + np.arange(256)
        ca = 1024 + np.arange(512)
        cg = 1536 + np.arange(3072)
        cols = np.concatenate([cq, cqp, ck, ckp, cs, cv, cv, ca, cg])
        assert cols.size == 5632
        w1 = w_in[:, cols]
        parts = []
        for g in range(11):
            parts.append(_pack(w1[:, g * 512:(g + 1) * 512]))
        wao = np.asarray(inp["w_attn_out"][l], f)
        for g in range(2):
            parts.append(_pack(wao[:, g * 512:(g + 1) * 512]))
        wco = np.asarray(inp["w_conv_out"][l], f)
        for g in range(2):
            parts.append(_pack(wco[:, g * 512:(g + 1) * 512]))
        bp = np.zeros((128, 16, 128), f)
        bre = np.asarray(inp["ssm_b_re"][l], f)
        bim = np.asarray(inp["ssm_b_im"][l], f)
        cre = np.asarray(inp["ssm_c_re"][l], f)
        cim = np.asarray(inp["ssm_c_im"][l], f)
        for gp in range(8):
            for g2 in range(2):
                g = 2 * gp + g2
                r0 = (gp % 4) * 32 + g2 * 16
                bp[r0:r0 + 16, gp * 2 + 0, g2 * 64:(g2 + 1) * 64] = bre[g].T
                bp[r0:r0 + 16, gp * 2 + 1, g2 * 64:(g2 + 1) * 64] = bim[g].T
                craw[l, g2 * 64:(g2 + 1) * 64, 0, gp, r0:r0 + 16] = cre[g].T
                craw[l, g2 * 64:(g2 + 1) * 64, 1, gp, r0:r0 + 16] = cim[g].T
        parts.append(bp.reshape(-1))
        wg = np.asarray(inp["w_ssm_glu"][l], f)
        for j in range(4):
            cc = np.concatenate([j * 256 + np.arange(256), 1024 + j * 256 + np.arange(256)])
            parts.append(_pack(wg[:, cc]))
        wmo = np.asarray(inp["w_mix_out"][l], f)
        for g in range(2):
            parts.append(_pack(wmo[:, g * 512:(g + 1) * 512]))
        wfi = np.asarray(inp["w_ffn_in"][l], f)
        for j in range(11):
            cc = np.concatenate([j * 256 + np.arange(256), 2816 + j * 256 + np.arange(256)])
            parts.append(_pack(wfi[:, cc]))
        wfo = np.asarray(inp["w_ffn_out"][l], f)
        for ng in range(2):
            for (k0, k1) in ((0, 8), (8, 16), (16, 22)):
                parts.append(_pack(wfo[k0 * 128:k1 * 128, ng * 512:(ng + 1) * 512]))
        wpi = np.asarray(inp["w_ple_in"][l], f)
        for g in range(2):
            parts.append(_pack(wpi[:, g * 512:(g + 1) * 512]))
        wpg = np.asarray(inp["w_ple_gate"][l], f)
        for g in range(2):
            parts.append(_pack(wpg[:, g * 512:(g + 1) * 512]))
        flat = np.concatenate(parts)
        assert flat.size == LW, (flat.size, LW)
        wf[l] = flat
        b = l * LS

        def put(name, arr):
            o, w = SOFF[name]
            assert arr.shape == (128, w), (name, arr.shape)
            small[:, b + o:b + o + w] = arr

        put("mixg", _chunked(np.asarray(inp["mix_norm_g"][l], f)))
        put("ffng", _chunked(np.asarray(inp["ffn_norm_g"][l], f)))
        put("pleg", _chunked(np.asarray(inp["ple_norm_g"][l], f)))
        put("bgate", _chunked(np.asarray(inp["b_gate"][l], f)))
        put("bglu", _chunked(np.asarray(inp["b_ssm_glu"][l], f)))
        cw = np.asarray(inp["conv_dw_w"][l], f)
        cwl = np.zeros((128, 62), f)
        for c in range(2):
            cwl[:, c * 31:(c + 1) * 31] = cw[:, c * 128:(c + 1) * 128].T
        put("convw", cwl)
        put("convb", _chunked(np.asarray(inp["conv_dw_b"][l], f)))
        put("lng", _chunked(np.asarray(inp["conv_norm_g"][l], f)))
        put("lnb", _chunked(np.asarray(inp["conv_norm_b"][l], f)))
        put("ssmd", _chunked(np.asarray(inp["ssm_d"][l], f)))
        put("sinks", np.broadcast_to(np.asarray(inp["attn_sinks"][l], f)[None, :], (128, 8)))
        lre = np.asarray(inp["ssm_lambda_re"][l], f)
        lim = np.asarray(inp["ssm_lambda_im"][l], f)
        ldt = np.asarray(inp["ssm_log_dt"][l], f)
        a1 = np.zeros((128, 8), f)
        a2 = np.zeros((128, 8), f)
        a3 = np.zeros((128, 8), f)
        for gp in range(8):
            for g2 in range(2):
                a1[g2 * 64:(g2 + 1) * 64, gp] = lre[2 * gp + g2]
                a2[g2 * 64:(g2 + 1) * 64, gp] = lim[2 * gp + g2]
                a3[g2 * 64:(g2 + 1) * 64, gp] = ldt[2 * gp + g2]
        put("lamre", a1)
        put("lamim", a2)
        put("logdt", a3)
    small[:, G_FINALG:G_FINALG + 8] = _chunked(np.asarray(inp["final_norm_g"], f))
    inv_freq = (np.float32(500000.0) ** (-np.arange(0, 16, 2, dtype=np.float32) / np.float32(16))).astype(f)
    for p in range(128):
        r = p % 64
        if r < 8:
            small[p, G_INVF] = inv_freq[r]
            small[p, G_SGN] = -1.0
        elif r < 16:
            small[p, G_INVF] = inv_freq[r - 8]
            small[p, G_SGN] = 1.0
    cst = np.zeros((128, NCST), f)
    cst[:, C_ID:C_ID + 128] = np.eye(128, dtype=f)
    qi = np.arange(128)[:, None]
    kj = np.arange(256)[None, :]
    dist = qi + 128 - kj
    ok = (dist >= 0) & (dist < 128)
    cst[:, C_MN:C_MN + 256] = np.where(ok, 0.0, NEG)
    cst[:, C_MF:C_MF + 256] = np.where(ok & (kj >= 128), 0.0, NEG)
    cst[:, C_TV:C_TV + 128] = np.arange(1, 129, dtype=f)[None, :]
    return wf, craw.reshape(DEPTH, 128, 2048), small, cst


class _Stop(Exception):
    pass


STOP = None
SKIP = set()
DBG = {}


_CNT = {}


def chk(name):
    if STOP is None:
        return
    nm, _, n = STOP.partition("@")
    if nm != name:
        return
    c = _CNT.get(name, 0)
    _CNT[name] = c + 1
    if c == int(n or 0):
        raise _Stop()


def build(n_seq, n_tiles, layers, final_norm=True, dbg=False):
    nc = bass.Bass("TRN2", target_bir_lowering=False)
    ntok = n_tiles * T
    x_d = nc.dram_tensor("x", [n_seq, ntok, D], F32, kind="ExternalInput").ap()
    p_d = nc.dram_tensor("p", [DEPTH, n_seq, ntok, 256], F32, kind="ExternalInput").ap()
    pos_d = nc.dram_tensor("pos", [n_seq, ntok], I32, kind="ExternalInput").ap()
    wf_d = nc.dram_tensor("wf", [DEPTH, LW], F32, kind="ExternalInput").ap()
    craw_d = nc.dram_tensor("craw", [DEPTH, 128, 2048], F32, kind="ExternalInput").ap()
    small_d = nc.dram_tensor("small", [128, NS], F32, kind="ExternalInput").ap()
    cst_d = nc.dram_tensor("cst", [128, NCST], F32, kind="ExternalInput").ap()
    out_d = nc.dram_tensor("out", [n_seq, ntok, D], F32, kind="ExternalOutput").ap()
    dbg_d = nc.dram_tensor("dbg", [128, 8192], F32, kind="ExternalOutput").ap() if DBG else None
    wscr = nc.dram_tensor("wscr", [DEPTH, LW], BF16, kind="Internal").ap()
    cscr = nc.dram_tensor("cscr", [DEPTH, 128, 3072], BF16, kind="Internal").ap()
    tscr = nc.dram_tensor("tscr", [DEPTH, 128, 2048], F32, kind="Internal").ap()
    dscr = nc.dram_tensor("dscr", [DEPTH, 2, 128, 31 * 128], BF16, kind="Internal").ap()

    with ExitStack() as st:
        def sb(name, shape, dt):
            return st.enter_context(nc.sbuf_tensor("sb_" + name, shape, dt))

        xT = sb("xT", [128, 8, T], F32)
        hT = sb("hT", [128, 8, T], BF16)
        rstd = sb("rstd", [128, T], F32)
        wbuf = sb("wbuf", [128, NBUF, WSLOT], BF16)
        gates = sb("gates", [128, 24, T], BF16)
        merged = sb("merged", [128, 8, T], F32)
        cosF = sb("cosF", [128, T], F32)
        sinF = sb("sinF", [128, T], F32)
        qtmp = sb("qtmp", [128, 2, T], F32)
        posi_t = sb("posi", [128, T], I32)
        qT = sb("qT", [128, 4, T], BF16)
        kT = sb("kT", [128, 2, 640], BF16)
        Vt = sb("Vt", [128, 5, 128], BF16)
        kcar = sb("kcar", [128, DEPTH, 2, 128], BF16)
        vcar = sb("vcar", [128, DEPTH, 128], BF16)
        Sm = sb("Sm", [128, 2, 2, 256], F32)
        Pb = sb("Pb", [128, 2, 2, 256], BF16)
        PTs = sb("PTs", [128, 2, 4, 128], BF16)
        stat = sb("stat", [128, 2, 16], F32)
        yattn = sb("yattn", [128, 4, T], BF16)
        uT32 = sb("uT32", [128, 2, T], F32)
        uTb = sb("uTb", [128, 2, T], BF16)
        ucb = sb("ucb", [128, 2, 30 + T], BF16)
        ucar = sb("ucar", [128, DEPTH, 2, 30], BF16)
        bpad_t = sb("bpad_t", [128, 16, 128], BF16)
        cpad_t = sb("cpad_t", [128, 24, 128], BF16)
        gT2 = sb("gT2", [128, 2, T], BF16)
        tmp = sb("tmp", [128, 4, T], F32)
        cacc = sb("cacc", [128, 2, T], F32)
        convact = sb("convact", [128, 2, T], BF16)
        w4 = sb("w4", [128, 2, 2, T], F32)
        r4 = sb("r4", [128, 2, 2, T], F32)
        Pp = sb("Pp", [128, 2, 4, T], BF16)
        tab = sb("tab", [128, 2, 8, 128], F32)
        scar = sb("scar", [128, DEPTH, 2, 8], F32)
        rho = sb("rho", [128, DEPTH, 8], F32)
        s8 = sb("s8", [128, 24, 8], F32)
        ptok = sb("ptok", [128, 4, 256], F32)
        ptokb = sb("ptokb", [128, 4, 256], BF16)
        pT = sb("pT", [128, 2, T], BF16)
        small = sb("small", [128, NS], F32)
        cst = sb("cst", [128, NCST], F32)
        identb = sb("identb", [128, 128], BF16)
        maskb = sb("maskb", [128, 2, 2, 256], BF16)
        onesb = sb("onesb", [128, 128], BF16)
        onesf = sb("onesf", [128, 128], F32)
        psf = st.enter_context(nc.psum_tensor("psf", [128, 6, T], F32))
        psb = st.enter_context(nc.psum_tensor("psb", [128, 2, 8, 128], BF16))

        act_ffn = gates[:, 0:22, :]
        ys = cacc
        ys_t = cacc
        gT = gT2
        xio = merged[:].rearrange("p c t -> p (c t)").rearrange("p (tb d) -> p tb d", tb=4)
        identf = cst[:, C_ID:C_ID + 128]
        maskN = cst[:, C_MN:C_MN + 256]
        maskF = cst[:, C_MF:C_MF + 256]
        tvec = cst[:, C_TV:C_TV + 128]

        S = Sched(nc)
        CH = 128 * 16384
        state = {"bank": 0, "wslot": 0, "alt": 0}

        held = set()

        def bank(hold=False):
            while True:
                b = state["bank"] % 6
                state["bank"] += 1
                if b not in held:
                    break
            if hold:
                held.add(b)
            return psf[:, b, :], "ps%d" % b

        def release(pk):
            held.discard(int(pk[2:]))

        def sm(l, name, c0=0, c1=None):
            o, w = SOFF[name]
            if c1 is None:
                c1 = w
            return small[:, l * LS + o + c0:l * LS + o + c1]

        def evac_engine():
            state["alt"] ^= 1
            return "act" if state["alt"] else "dve"

        def copy_op(eng, out, in_, reads, writes):
            if eng == "act":
                S.op("act", lambda e: e.copy(out, in_), reads=reads, writes=writes)
            else:
                S.op(eng, lambda e: e.tensor_copy(out, in_), reads=reads, writes=writes)

        def wload(l, piece, src=None, srckey=None):
            off, kcp, ncols = piece
            slot = state["wslot"] % NBUF
            state["wslot"] += 1
            n = kcp * ncols
            view = wbuf[:, slot, 0:n]
            if src is None:
                src = wscr[l, off:off + 128 * n].rearrange("(p f) -> p f", p=128)
                srckeys = ["wscr%d_%d" % (l, ci) for ci in range(off // CH, (off + 128 * n - 1) // CH + 1)]
            else:
                srckeys = [srckey]
            S.dma("sp", lambda e: e.dma_start(out=view, in_=src), "w%d" % slot,
                  reads=srckeys, writes=["w%d" % slot])
            return view.rearrange("p (k n) -> p k n", k=kcp), "w%d" % slot

        def mm_chain(out, pairs, reads, wkey):
            def fn(e):
                n = len(pairs)
                ins = None
                for i, (a, b) in enumerate(pairs):
                    ins = e.matmul(out, a, b, start=(i == 0), stop=(i == n - 1))
                return ins
            S.op("pe", fn, reads=reads, writes=[wkey])


        def early_exit():
            S.op("dve", lambda e: e.memset(w4[:], 1.0), writes=["w4"])
            S.dma("sp", lambda e: e.dma_start(out=out_d[0, 0:128, :], in_=w4[:].rearrange("p a b t -> p (a b t)")[:, 0:1024]),
                  "out", reads=["w4"], writes=["out"])
            S.final_wait("sp", ["out", "small", "cst"] + ["pp%d_%d" % (l, ci) for l in layers for ci in range(9)] + ["tscr%d" % l for l in layers] + ["cscr%d" % l for l in layers])
            S.emit(st)
            return nc
        def dump(name, ap, key, ncols, tile_sel=None):
            if name not in DBG:
                return
            c0, tsel = DBG[name]
            if tsel is not None and tsel != tile_sel:
                return
            S.dma("pool", lambda e: e.dma_start(out=dbg_d[:, c0:c0 + ncols], in_=ap), "dbg_" + name, reads=[key], writes=["dbg"])

        S.dma("sp", lambda e: e.dma_start(out=small[:], in_=small_d[:, :]), "small", writes=["small"])
        S.dma("sp", lambda e: e.dma_start(out=cst[:], in_=cst_d[:, :]), "cst", writes=["cst"])
        for l in layers:
            o = 0
            while o < LW:
                n = min(CH, LW - o)
                S.dma("pool", lambda e, l=l, o=o, n=n: e.dma_start(
                    out=wscr[l, o:o + n].rearrange("(p f) -> p f", p=128),
                    in_=wf_d[l, o:o + n].rearrange("(p f) -> p f", p=128)),
                    "pp%d_%d" % (l, o // CH), writes=["wscr%d_%d" % (l, o // CH)])
                o += n
        if STOP == "setup0":
            return early_exit()
        S.op("dve", lambda e: e.tensor_copy(identb[:], identf), reads=["cst"], writes=["identb"])
        S.op("dve", lambda e: e.memset(onesb[:], 1.0), writes=["onesb"])
        for h_ in range(2):
            S.op("dve", lambda e, h_=h_: e.tensor_copy(maskb[:, 0, h_, :], maskN), reads=["cst"], writes=["maskb"])
            S.op("dve", lambda e, h_=h_: e.tensor_copy(maskb[:, 1, h_, :], maskF), reads=["cst"], writes=["maskb"])
        S.op("dve", lambda e: e.memset(onesf[:], 1.0 / 256.0), writes=["onesf"])
        S.op("dve", lambda e: e.memset(kcar[:], 0.0), writes=["kcar"])
        S.op("dve", lambda e: e.memset(vcar[:], 0.0), writes=["vcar"])
        S.op("dve", lambda e: e.memset(kT[:], 0.0), writes=["kT"])
        S.op("dve", lambda e: e.memset(Vt[:], 0.0), writes=["Vt"])

        def sin_of(out_ap, a_ap, shift, tv, tk, rk, wk, tvk, tkk):
            S.op("dve", lambda e: e.tensor_scalar(tv, a_ap, 1.0 / TWO_PI, shift / TWO_PI, ALU.mult, ALU.add),
                 reads=list(rk), writes=[tvk])
            S.op("dve", lambda e: e.tensor_scalar(tk, tv, MAGIC, None, ALU.add), reads=[tvk], writes=[tkk])
            S.op("dve", lambda e: e.tensor_scalar(tk, tk, MAGIC, None, ALU.subtract), reads=[tkk], writes=[tkk])
            S.op("dve", lambda e: e.tensor_tensor(tv, tv, tk, ALU.subtract), reads=[tvk, tkk], writes=[tvk])
            S.op("dve", lambda e: e.tensor_scalar(tv, tv, 0.49999, -0.49999, ALU.min, ALU.max), reads=[tvk], writes=[tvk])
            S.op("act", lambda e: e.activation(out_ap, tv, AF.Sin, scale=TWO_PI), reads=[tvk], writes=list(wk))

        w4f = w4[:].rearrange("p a b t -> p (a b t)")
        r4f = r4[:].rearrange("p a b t -> p (a b t)")
        tabf = tab[:].rearrange("p a g t -> p (a g t)")
        for l in layers:
            def s8v(i):
                return s8[:, i, :]
            lamre, lamim, logdt = sm(l, "lamre"), sm(l, "lamim"), sm(l, "logdt")
            DT, LR, TH, SN, CS, ARE, AIM, XRE, T1, T2, RD, FRE, FIM, NFRE, NFIM, TV, TK = [s8v(i) for i in range(17)]
            S.op("act", lambda e: e.activation(DT, logdt, AF.Exp), reads=["small"], writes=["s8"])
            S.op("dve", lambda e: e.tensor_scalar(LR, lamre, -1e-4, None, ALU.min), reads=["small"], writes=["s8"])
            S.op("dve", lambda e: e.tensor_tensor(T1, LR, DT, ALU.mult), reads=["s8"], writes=["s8"])
            S.op("act", lambda e, l=l: e.activation(rho[:, l, :], T1, AF.Exp), reads=["s8"], writes=["rho"])
            S.op("dve", lambda e: e.tensor_tensor(TH, lamim, DT, ALU.mult), reads=["s8", "small"], writes=["s8"])
            sin_of(SN, TH, 0.0, TV, TK, ["s8"], ["s8"], "s8", "s8")
            sin_of(CS, TH, math.pi / 2, TV, TK, ["s8"], ["s8"], "s8", "s8")
            S.op("dve", lambda e, l=l: e.tensor_tensor(ARE, rho[:, l, :], CS, ALU.mult), reads=["s8", "rho"], writes=["s8"])
            S.op("dve", lambda e, l=l: e.tensor_tensor(AIM, rho[:, l, :], SN, ALU.mult), reads=["s8", "rho"], writes=["s8"])
            S.op("dve", lambda e: e.tensor_scalar(XRE, ARE, -1.0, None, ALU.add), reads=["s8"], writes=["s8"])
            S.op("dve", lambda e: e.tensor_tensor(T1, LR, LR, ALU.mult), reads=["s8"], writes=["s8"])
            S.op("dve", lambda e: e.tensor_tensor(T2, lamim, lamim, ALU.mult), reads=["s8", "small"], writes=["s8"])
            S.op("dve", lambda e: e.tensor_tensor(T1, T1, T2, ALU.add), reads=["s8"], writes=["s8"])
            S.op("dve", lambda e: e.reciprocal(RD, T1), reads=["s8"], writes=["s8"])
            S.op("dve", lambda e: e.tensor_tensor(T1, XRE, LR, ALU.mult), reads=["s8"], writes=["s8"])
            S.op("dve", lambda e: e.tensor_tensor(T2, AIM, lamim, ALU.mult), reads=["s8", "small"], writes=["s8"])
            S.op("dve", lambda e: e.tensor_tensor(T1, T1, T2, ALU.add), reads=["s8"], writes=["s8"])
            S.op("dve", lambda e: e.tensor_tensor(FRE, T1, RD, ALU.mult), reads=["s8"], writes=["s8"])
            S.op("dve", lambda e: e.tensor_tensor(T1, AIM, LR, ALU.mult), reads=["s8"], writes=["s8"])
            S.op("dve", lambda e: e.tensor_tensor(T2, XRE, lamim, ALU.mult), reads=["s8", "small"], writes=["s8"])
            S.op("dve", lambda e: e.tensor_tensor(T1, T1, T2, ALU.subtract), reads=["s8"], writes=["s8"])
            S.op("dve", lambda e: e.tensor_tensor(FIM, T1, RD, ALU.mult), reads=["s8"], writes=["s8"])
            S.op("dve", lambda e: e.tensor_scalar(NFRE, FRE, -1.0, None, ALU.mult), reads=["s8"], writes=["s8"])
            S.op("dve", lambda e: e.tensor_scalar(NFIM, FIM, -1.0, None, ALU.mult), reads=["s8"], writes=["s8"])
            for gp in range(8):
                S.op("dve", lambda e, gp=gp: e.tensor_scalar(w4f[:, gp * 128:(gp + 1) * 128], tvec, TH[:, gp:gp + 1], None, ALU.mult),
                     reads=["s8", "cst"], writes=["w4"])
            sin_of(tabf[:, 1024:2048], w4f[:, 0:1024], 0.0, w4f[:, 1024:2048], r4f[:, 0:1024], ["w4"], ["tab"], "w4", "r4")
            sin_of(tabf[:, 0:1024], w4f[:, 0:1024], math.pi / 2, w4f[:, 1024:2048], r4f[:, 0:1024], ["w4"], ["tab"], "w4", "r4")
            S.dma("sp", lambda e, l=l: e.dma_start(out=tscr[l, :, :], in_=tabf), "tscr%d" % l, reads=["tab"], writes=["tscr%d" % l])
            crawt = merged[:].rearrange("p c t -> p (c t)")[:, 0:2048].rearrange("p (r g n) -> p r g n", r=2, g=8)
            cpad = hT[:].rearrange("p c t -> p (c t)")[:, 0:3072].rearrange("p (v n) -> p v n", n=128)
            tq = qtmp[:, 0, 0:128]
            S.dma("sp", lambda e, l=l: e.dma_start(out=merged[:].rearrange("p c t -> p (c t)")[:, 0:2048], in_=craw_d[l, :, :]),
                  "craw", writes=["merged"])
            for gp in range(8):
                cre_, cim_ = crawt[:, 0, gp, :], crawt[:, 1, gp, :]
                S.op("dve", lambda e, gp=gp, cim_=cim_: e.tensor_scalar(tq, cim_, FIM[:, gp:gp + 1], None, ALU.mult),
                     reads=["merged", "s8"], writes=["qtmp"])
                S.op("dve", lambda e, gp=gp, cre_=cre_: e.scalar_tensor_tensor(cpad[:, gp * 3 + 0, :], cre_, FRE[:, gp:gp + 1], tq, ALU.mult, ALU.subtract),
                     reads=["merged", "s8", "qtmp"], writes=["hT"])
                S.op("dve", lambda e, gp=gp, cre_=cre_: e.scalar_tensor_tensor(cpad[:, gp * 3 + 1, :], cre_, NFRE[:, gp:gp + 1], tq, ALU.mult, ALU.add),
                     reads=["merged", "s8", "qtmp"], writes=["hT"])
                S.op("dve", lambda e, gp=gp, cim_=cim_: e.tensor_scalar(tq, cim_, NFRE[:, gp:gp + 1], None, ALU.mult),
                     reads=["merged", "s8", "hT"], writes=["qtmp"])
                S.op("dve", lambda e, gp=gp, cre_=cre_: e.scalar_tensor_tensor(cpad[:, gp * 3 + 2, :], cre_, NFIM[:, gp:gp + 1], tq, ALU.mult, ALU.add),
                     reads=["merged", "s8", "qtmp"], writes=["hT"])
            S.dma("sp", lambda e, l=l: e.dma_start(out=cscr[l, :, :], in_=hT[:].rearrange("p c t -> p (c t)")[:, 0:3072]),
                  "cscr%d" % l, reads=["hT"], writes=["cscr%d" % l])
            dstage = gates[:].rearrange("p c t -> p (c t)")[:, 0:62 * 128].rearrange("p (k n) -> p k n", n=128)
            for k in range(62):
                S.op("dve", lambda e, k=k, l=l: e.tensor_scalar(dstage[:, k, :], identb[:], sm(l, "convw", k, k + 1), None, ALU.mult),
                     reads=["identb", "small"], writes=["gates"])
            for c in range(2):
                S.dma("sp", lambda e, l=l, c=c: e.dma_start(out=dscr[l, c, :, :], in_=gates[:].rearrange("p c t -> p (c t)")[:, c * 3968:(c + 1) * 3968]),
                      "dscr%d" % l, reads=["gates"], writes=["dscr%d" % l])

        if STOP == "setup1":
            return early_exit()
        def sqbuf(m):
            return (qT[:, m, :], "qT") if m < 4 else (yattn[:, m - 4, :], "yattn")

        def sq_chunk(m):
            dst, dk = sqbuf(m)
            S.op("act", lambda e: e.activation(dst, xT[:, m, :], AF.Square), reads=["xT"], writes=[dk])

        def rmsnorm_to_hT(gain_cols, out_f32=None):
            ps, pk = bank()
            mm_chain(ps, [(onesb[:], sqbuf(c)[0]) for c in range(8)], ["onesb", "qT", "yattn"], pk)
            S.op("act", lambda e: e.activation(rstd[:], ps, AF.Sqrt, scale=1.0 / D, bias=EPS), reads=[pk], writes=["rstd"])
            S.op("dve", lambda e: e.reciprocal(rstd[:], rstd[:]), reads=["rstd"], writes=["rstd"])
            for c in range(8):
                dst = hT[:, c, :] if out_f32 is None else out_f32[:, c, :]
                S.op("dve", lambda e, c=c, dst=dst: e.scalar_tensor_tensor(dst, xT[:, c, :], gain_cols[:, c:c + 1], rstd[:], ALU.mult, ALU.mult),
                     reads=["xT", "rstd", "small"], writes=["hT" if out_f32 is None else "merged"])

        def proj4(wv, wk, rhs_tile, rhs_key, kcn, chunks, consumer):
            for i in chunks:
                ps, pk = bank()
                mm_chain(ps, [(wv[:, k, i * 128:(i + 1) * 128], rhs_tile[:, k, :]) for k in range(kcn)], [wk, rhs_key], pk)
                consumer(i, ps, pk)

        for s in range(n_seq):
            for j in range(n_tiles):
                t0 = j * T
                first = (j == 0)
                S.dma("sp", lambda e, s=s, t0=t0: e.dma_start(out=xio, in_=x_d[s, t0:t0 + T, :].rearrange("(tb p) d -> p tb d", p=128)),
                      "xio", writes=["merged"])
                for c in range(8):
                    ps, pk = bank()

                    def tr(e, c=c, ps=ps):
                        ins = None
                        for tb in range(4):
                            ins = e.transpose(ps[:, tb * 128:(tb + 1) * 128], xio[:, tb, c * 128:(c + 1) * 128], identf)
                        return ins
                    S.op("pe", tr, reads=["merged", "cst"], writes=[pk])
                    copy_op(evac_engine(), xT[:, c, :], ps, [pk], ["xT"])
                    sq_chunk(c)
                t1a = (STOP == "t1a" and j == 1)
                if "rope" not in SKIP and not t1a:
                    posi = posi_t[:]
                    S.dma("sp", lambda e, s=s, t0=t0: e.dma_start(out=posi, in_=pos_d[s:s + 1, t0:t0 + T].partition_broadcast(128)),
                          "posi", writes=["posi"])
                    S.op("dve", lambda e: e.tensor_copy(qtmp[:, 1, :], posi), reads=["posi"], writes=["qtmp1"])
                    S.op("dve", lambda e: e.tensor_scalar(qtmp[:, 1, :], qtmp[:, 1, :], small[:, G_INVF:G_INVF + 1], None, ALU.mult),
                         reads=["qtmp1", "small"], writes=["qtmp1"])
                    sin_of(cosF[:], qtmp[:, 1, :], math.pi / 2, tmp[:, 0, :], tmp[:, 1, :], ["qtmp1"], ["cosF"], "tmp0", "tmp1")
                    sin_of(sinF[:], qtmp[:, 1, :], 0.0, tmp[:, 0, :], tmp[:, 1, :], ["qtmp1"], ["sinF"], "tmp0", "tmp1")
                    S.op("dve", lambda e: e.tensor_scalar(sinF[:], sinF[:], small[:, G_SGN:G_SGN + 1], None, ALU.mult),
                         reads=["sinF", "small"], writes=["sinF"])

                stopped = False
                for l in ([] if t1a else layers):
                  try:
                      chk("tile0")
                      S.dma("sp", lambda e, l=l, s=s, t0=t0: e.dma_start(out=ptok[:], in_=p_d[l, s, t0:t0 + T, :].rearrange("(tb p) d -> p tb d", p=128)),
                            "ptok", writes=["ptok"])
                      S.op("act", lambda e: e.copy(ptokb[:], ptok[:]), reads=["ptok"], writes=["ptokb"])
                      for c in range(2):
                          def trp(e, c=c):
                              ins = None
                              for tb in range(4):
                                  ins = e.transpose(psb[:, c, tb, :], ptokb[:, tb, c * 128:(c + 1) * 128], identb[:])
                              return ins
                          S.op("pe", trp, reads=["ptokb", "identb"], writes=["psb%d" % c])
                          S.op("dve", lambda e, c=c: e.tensor_copy(pT[:, c, :].rearrange("p (tb t) -> p tb t", tb=4), psb[:, c, 0:4, :]), reads=["psb%d" % c], writes=["pT"])
                      rmsnorm_to_hT(sm(l, "mixg"))
                      chk("p0")
                      if first:
                          S.op("pool", lambda e, l=l: e.memset(ucar[:, l, :, :], 0.0), writes=["ucar"])
                          S.op("pool", lambda e, l=l: e.memset(scar[:, l, :, :], 0.0), writes=["scar"])
                      S.op("pool", lambda e, l=l: e.tensor_copy(kT[:, :, 0:128], kcar[:, l, :, :]), reads=["kcar"], writes=["kT"])
                      S.op("pool", lambda e, l=l: e.tensor_copy(Vt[:, 0, :], vcar[:, l, :]), reads=["vcar"], writes=["Vt"])
                      S.op("pool", lambda e, l=l: e.tensor_copy(ucb[:, :, 0:30], ucar[:, l, :, :]), reads=["ucar"], writes=["ucb"])
                      S.dma("sp", lambda e, l=l: e.dma_start(out=bpad_t[:].rearrange("p a n -> p (a n)"),
                                                               in_=wscr[l, PIECES["bpad"][0][0]:PIECES["bpad"][0][0] + 128 * 2048].rearrange("(p f) -> p f", p=128)),
                            "bpad", reads=["wscr%d_%d" % (l, ci) for ci in range(PIECES["bpad"][0][0] // CH, (PIECES["bpad"][0][0] + 128 * 2048 - 1) // CH + 1)], writes=["bpad"])
                      S.dma("sp", lambda e, l=l: e.dma_start(out=cpad_t[:].rearrange("p a n -> p (a n)"), in_=cscr[l, :, :]), "cpad",
                            reads=["cscr%d" % l], writes=["cpad"])
                      S.dma("sp", lambda e, l=l: e.dma_start(out=tabf, in_=tscr[l, :, :]), "tab", reads=["tscr%d" % l], writes=["tab"])
                      chk("p1")
                      win = PIECES["win"]
                      sinks = sm(l, "sinks")

                      def rope_pair(wA, wkA, iA, wB, wkB, iB, dst, dkey):
                          ps, pk = bank()
                          mm_chain(ps, [(wA[:, k, iA * 128:(iA + 1) * 128], hT[:, k, :]) for k in range(8)], [wkA, "hT"], pk)
                          S.op("dve", lambda e: e.tensor_tensor(qtmp[:, 0, :], ps, cosF[:], ALU.mult), reads=[pk, "cosF"], writes=["qtmp"])
                          ps2, pk2 = bank()
                          mm_chain(ps2, [(wB[:, k, iB * 128:(iB + 1) * 128], hT[:, k, :]) for k in range(8)], [wkB, "hT"], pk2)
                          S.op("dve", lambda e: e.tensor_tensor(qtmp[:, 1, :], ps2, sinF[:], ALU.mult), reads=[pk2, "sinF"], writes=["qtmp1"])
                          S.op("dve", lambda e: e.tensor_tensor(dst, qtmp[:, 0, :], qtmp[:, 1, :], ALU.add), reads=["qtmp", "qtmp1"], writes=[dkey])

                      wv0, wk0 = wload(l, win[0])
                      wv1, wk1 = wload(l, win[1])
                      for i in range(4):
                          rope_pair(wv0, wk0, i, wv1, wk1, i, qT[:, i, :], "qT")
                      chk("p2")
                      wv2, wk2 = wload(l, win[2])
                      for i in range(2):
                          rope_pair(wv2, wk2, i, wv2, wk2, i + 2, kT[:, i, 128:640], "kT")
                      chk("p3")
                      wv3, wk3 = wload(l, win[3])
                      for i in range(2):
                          ps, pk = bank()
                          mm_chain(ps, [(wv3[:, k, i * 128:(i + 1) * 128], hT[:, k, :]) for k in range(8)], [wk3, "hT"], pk)
                          S.op("act", lambda e: e.copy(uT32[:, i, :], ps), reads=[pk], writes=["uT32"])
                          S.op("dve", lambda e: e.tensor_copy(uTb[:, i, :], uT32[:, i, :]), reads=["uT32"], writes=["uTb"])
                      ps, pk = bank()

                      def vproj(e):
                          ins = None
                          for tb in range(4):
                              for k in range(8):
                                  ins = e.matmul(ps[:, tb * 128:(tb + 1) * 128], hT[:, k, tb * 128:(tb + 1) * 128], wv3[:, k, 256:384],
                                                 start=(k == 0), stop=(k == 7))
                          return ins
                      S.op("pe", vproj, reads=[wk3, "hT"], writes=[pk])
                      S.op("act", lambda e: e.copy(Vt[:, 1:5, :], ps.rearrange("p (tb d) -> p tb d", tb=4)), reads=[pk], writes=["Vt"])
                      chk("p4")
                      wv4, wk4 = wload(l, win[4])
                      for i in range(2):
                          psg, pkg = bank()
                          mm_chain(psg, [(wv4[:, k, (i + 2) * 128:(i + 3) * 128], hT[:, k, :]) for k in range(8)], [wk4, "hT"], pkg)
                          S.op("act", lambda e: e.activation(tmp[:, 2, :], psg, AF.Sigmoid), reads=[pkg], writes=["tmp2"])
                          psa, pka = bank()
                          mm_chain(psa, [(wv4[:, k, i * 128:(i + 1) * 128], hT[:, k, :]) for k in range(8)], [wk4, "hT"], pka)
                          S.op("dve", lambda e: e.tensor_tensor(ucb[:, i, 30:30 + T], psa, tmp[:, 2, :], ALU.mult), reads=[pka, "tmp2"], writes=["ucb"])
                      chk("p5")
                      for c in range(2):
                          wvd, wkd = wload(l, (0, 31, 128), src=dscr[l, c, :, :], srckey="dscr%d" % l)
                          ps, pk = bank()
                          mm_chain(ps, [(wvd[:, k, :], ucb[:, c, k:k + T]) for k in range(31)], [wkd, "ucb"], pk)
                          S.op("act", lambda e: e.activation(cacc[:, c, :], ps, AF.Identity, bias=sm(l, "convb", c, c + 1)), reads=[pk, "small"], writes=["cacc%d" % c])
                      S.op("pool", lambda e, l=l: e.tensor_copy(ucar[:, l, :, :], ucb[:, :, T:T + 30]), reads=["ucb"], writes=["ucar"])
                      chk("proj")
                      psm, pkm = bank()
                      mm_chain(psm, [(onesf[:], cacc[:, c, :]) for c in range(2)], ["onesf", "cacc0", "cacc1"], pkm)
                      for c in range(2):
                          S.op("act", lambda e: e.activation(tmp[:, c, :], cacc[:, c, :], AF.Square), reads=["cacc%d" % c], writes=["tmp%d" % c])
                      psq, pkq = bank()
                      mm_chain(psq, [(onesf[:], tmp[:, c, :]) for c in range(2)], ["onesf", "tmp0", "tmp1"], pkq)
                      S.op("act", lambda e: e.copy(tmp[:, 2, :], psm), reads=[pkm], writes=["tmp2"])
                      S.op("dve", lambda e: e.tensor_tensor(tmp[:, 3, :], tmp[:, 2, :], tmp[:, 2, :], ALU.mult), reads=["tmp2"], writes=["tmp3"])
                      S.op("dve", lambda e: e.tensor_tensor(tmp[:, 3, :], psq, tmp[:, 3, :], ALU.subtract), reads=[pkq, "tmp3"], writes=["tmp3"])
                      S.op("act", lambda e: e.activation(tmp[:, 3, :], tmp[:, 3, :], AF.Sqrt, bias=EPS), reads=["tmp3"], writes=["tmp3"])
                      S.op("dve", lambda e: e.reciprocal(tmp[:, 3, :], tmp[:, 3, :]), reads=["tmp3"], writes=["tmp3"])
                      for c in range(2):
                          S.op("dve", lambda e: e.tensor_tensor(cacc[:, c, :], cacc[:, c, :], tmp[:, 2, :], ALU.subtract),
                               reads=["cacc%d" % c, "tmp2"], writes=["cacc%d" % c])
                          S.op("dve", lambda e: e.tensor_tensor(cacc[:, c, :], cacc[:, c, :], tmp[:, 3, :], ALU.mult),
                               reads=["cacc%d" % c, "tmp3"], writes=["cacc%d" % c])
                          S.op("act", lambda e: e.activation(convact[:, c, :], cacc[:, c, :], AF.Silu, scale=sm(l, "lng", c, c + 1), bias=sm(l, "lnb", c, c + 1)),
                               reads=["cacc%d" % c, "small"], writes=["convact"])

                      v4 = lambda ap: ap.rearrange("p (s t) -> p s t", s=4)
                      psy = [None, None]

                      def ssm_front(bi):
                          cu = bi // 2
                          gp0 = bi * 2
                          X = w4 if bi % 2 == 0 else r4
                          xk = "w4" if bi % 2 == 0 else "r4"
                          for g2 in range(2):
                              gp = gp0 + g2
                              psr, pkr = bank()
                              mm_chain(psr, [(bpad_t[:, gp * 2 + 0, :], uTb[:, cu, :])], ["bpad", "uTb"], pkr)
                              psi, pki = bank()
                              mm_chain(psi, [(bpad_t[:, gp * 2 + 1, :], uTb[:, cu, :])], ["bpad", "uTb"], pki)
                              ct = tab[:, 0, gp, :].unsqueeze(1).to_broadcast([128, 4, 128])
                              stt = tab[:, 1, gp, :].unsqueeze(1).to_broadcast([128, 4, 128])
                              wre, wim = v4(X[:, 0, g2, :]), v4(X[:, 1, g2, :])
                              t0v, t1v = v4(tmp[:, 0, :]), v4(tmp[:, 1, :])
                              S.op("dve", lambda e: e.tensor_tensor(t0v, v4(psr), ct, ALU.mult), reads=[pkr, "tab"], writes=["tmp0"])
                              S.op("dve", lambda e: e.tensor_tensor(t1v, v4(psi), stt, ALU.mult), reads=[pki, "tab"], writes=["tmp1"])
                              S.op("dve", lambda e: e.tensor_tensor(wre, t0v, t1v, ALU.add), reads=["tmp0", "tmp1"], writes=[xk])
                              S.op("dve", lambda e: e.tensor_tensor(t0v, v4(psi), ct, ALU.mult), reads=[pki, "tab", xk], writes=["tmp0"])
                              S.op("dve", lambda e: e.tensor_tensor(t1v, v4(psr), stt, ALU.mult), reads=[pkr, "tab", xk], writes=["tmp1"])
                              S.op("dve", lambda e: e.tensor_tensor(wim, t0v, t1v, ALU.subtract), reads=["tmp0", "tmp1"], writes=[xk])
                          for sub in range(4):
                              c0, c1 = sub * 128, (sub + 1) * 128
                              for g2 in range(2):
                                  gp = gp0 + g2
                                  for ri in range(2):
                                      S.op("dve", lambda e: e.tensor_tensor_scan(
                                          X[:, ri, g2, c0:c1], rho[:, l, gp:gp + 1].to_broadcast([128, 128]), X[:, ri, g2, c0:c1],
                                          scar[:, l, ri, gp:gp + 1], ALU.mult, ALU.add),
                                          reads=[xk, "rho", "scar"], writes=[xk])
                              rre_e, rim_e = X[:, 0, :, c1 - 1], X[:, 1, :, c1 - 1]
                              ct_e, st_e = tab[:, 0, gp0:gp0 + 2, 127], tab[:, 1, gp0:gp0 + 2, 127]
                              A_, B_ = s8[:, 20, 0:2], s8[:, 21, 0:2]
                              S.op("dve", lambda e: e.tensor_tensor(A_, ct_e, rre_e, ALU.mult), reads=[xk, "tab"], writes=["s8a"])
                              S.op("dve", lambda e: e.tensor_tensor(B_, st_e, rim_e, ALU.mult), reads=[xk, "tab"], writes=["s8b"])
                              S.op("dve", lambda e: e.tensor_tensor(scar[:, l, 0, gp0:gp0 + 2], A_, B_, ALU.subtract), reads=["s8a", "s8b"], writes=["scar"])
                              S.op("dve", lambda e: e.tensor_tensor(A_, st_e, rre_e, ALU.mult), reads=[xk, "tab", "scar"], writes=["s8a"])
                              S.op("dve", lambda e: e.tensor_tensor(B_, ct_e, rim_e, ALU.mult), reads=[xk, "tab", "scar"], writes=["s8b"])
                              S.op("dve", lambda e: e.tensor_tensor(scar[:, l, 1, gp0:gp0 + 2], A_, B_, ALU.add), reads=["s8a", "s8b"], writes=["scar"])

                      def ssm_demod(bi):
                          gp0 = bi * 2
                          X = w4 if bi % 2 == 0 else r4
                          xk = "w4" if bi % 2 == 0 else "r4"
                          for g2 in range(2):
                              gp = gp0 + g2
                              u = gp % 2
                              ct = tab[:, 0, gp, :].unsqueeze(1).to_broadcast([128, 4, 128])
                              stt = tab[:, 1, gp, :].unsqueeze(1).to_broadcast([128, 4, 128])
                              rre, rim = v4(X[:, 0, g2, :]), v4(X[:, 1, g2, :])
                              for pi, (ta, rb) in enumerate(((ct, rre), (stt, rim), (stt, rre), (ct, rim))):
                                  eng = "pool"
                                  S.op(eng, lambda e: e.tensor_tensor(v4(Pp[:, u, pi, :]), rb, ta, ALU.mult), reads=[xk, "tab"], writes=["Pp%d" % u])

                      def ssm_back(bi):
                          cu = bi // 2
                          if bi % 2 == 0:
                              psy[cu] = bank(hold=True)
                          ps_, pk_ = psy[cu]
                          for g2 in range(2):
                              gp = bi * 2 + g2
                              u = gp % 2
                              first_mm = (gp % 4 == 0)
                              last_mm = (gp % 4 == 3)

                              def cmm(e):
                                  ins = None
                                  for pi, var in enumerate((0, 1, 2, 2)):
                                      ins = e.matmul(ps_, cpad_t[:, gp * 3 + var, :], Pp[:, u, pi, :],
                                                     start=(first_mm and pi == 0), stop=(last_mm and pi == 3))
                                  return ins
                              S.op("pe", cmm, reads=["Pp%d" % u, "cpad"], writes=[pk_])

                      def gates_group(g):
                          wvg, wkg = wload(l, win[5 + g])
                          for i in range(4):
                              ci = g * 4 + i
                              ps, pk = bank()
                              mm_chain(ps, [(wvg[:, k, i * 128:(i + 1) * 128], hT[:, k, :]) for k in range(8)], [wkg, "hT"], pk)
                              S.op("act", lambda e: e.activation(gates[:, ci, :], ps, AF.Sigmoid, bias=sm(l, "bgate", ci, ci + 1)),
                                   reads=[pk, "small"], writes=["gates"])

                      psos = [None] * 4
                      psk = [None, None]

                      def attn_stage_a(un):
                          qb, kv, hh, u = un
                          hps = (2 * kv, 2 * kv + 1)
                          pss, pks = bank()
                          psk[u] = (pss, pks)
                          mi = 1 if (first and qb == 0) else 0

                          def scores(e):
                              ins = e.matmul(pss, identb[:], maskb[:, mi, :, :].rearrange("p h k -> p (h k)"), start=True, stop=False)
                              for i, hp in enumerate(hps):
                                  ins = e.matmul(pss[:, i * 256:(i + 1) * 256],
                                                 qT[hh * 64:(hh + 1) * 64, hp, qb * 128:(qb + 1) * 128],
                                                 kT[hh * 64:(hh + 1) * 64, kv, qb * 128:qb * 128 + 256], start=False, stop=(i == 1))
                              return ins
                          S.op("pe", scores, reads=["qT", "kT", "identb", "maskb"], writes=[pks])
                          stk = "stat%d" % u
                          h0 = 4 * kv + hh
                          snk = sinks[:, h0:h0 + 3:2]
                          S.op("dve", lambda e: e.tensor_reduce(stat[:, u, 0:2], pss.rearrange("p (h k) -> p h k", h=2), AX.X, ALU.max), reads=[pks], writes=[stk])
                          S.op("dve", lambda e: e.scalar_tensor_tensor(stat[:, u, 2:4], stat[:, u, 0:2], 0.125, snk, ALU.mult, ALU.max),
                               reads=[stk, "small"], writes=[stk])
                          S.op("dve", lambda e: e.tensor_scalar(stat[:, u, 4:6], stat[:, u, 2:4], -1.0, None, ALU.mult), reads=[stk], writes=[stk])
                          S.op("dve", lambda e: e.tensor_tensor(stat[:, u, 6:8], snk, stat[:, u, 2:4], ALU.subtract), reads=[stk, "small"], writes=[stk])

                      def attn_stage_b(un):
                          qb, kv, hh, u = un
                          pbk, stk = "Pb%d" % u, "stat%d" % u
                          pss, pks = psk[u]
                          for i in range(2):
                              S.op("act", lambda e: e.activation(Pb[:, u, i, :], pss[:, i * 256:(i + 1) * 256], AF.Exp, scale=0.125,
                                                                 bias=stat[:, u, 4 + i:5 + i], accum_out=stat[:, u, 8 + i:9 + i]),
                                   reads=[pks, stk], writes=[pbk, stk])
                          S.op("act", lambda e: e.activation(stat[:, u, 10:12], stat[:, u, 6:8], AF.Exp), reads=[stk], writes=[stk])
                          S.op("dve", lambda e: e.tensor_tensor(stat[:, u, 12:14], stat[:, u, 8:10], stat[:, u, 10:12], ALU.add), reads=[stk], writes=[stk])
                          S.op("dve", lambda e: e.reciprocal(stat[:, u, 14:16], stat[:, u, 12:14]), reads=[stk], writes=[stk])
                          for i in range(2):
                              S.op("act", lambda e: e.activation(Pb[:, u, i, :], Pb[:, u, i, :], AF.Copy, scale=stat[:, u, 14 + i:15 + i]),
                                   reads=[pbk, stk], writes=[pbk])

                      def attn_stage_c(un):
                          qb, kv, hh, u = un
                          hps = (2 * kv, 2 * kv + 1)
                          pbk, ptk = "Pb%d" % u, "PTs%d" % u

                          def ptrans(e):
                              ins = None
                              for i in range(2):
                                  for kb in range(2):
                                      ins = e.transpose(psb[:, u, i * 2 + kb, :], Pb[:, u, i, kb * 128:(kb + 1) * 128], identb[:])
                              return ins
                          S.op("pe", ptrans, reads=[pbk, "identb"], writes=["psb%d" % u])
                          S.op("act", lambda e: e.copy(PTs[:, u, :, :], psb[:, u, 0:4, :]), reads=["psb%d" % u], writes=[ptk])
                          for i, hp in enumerate(hps):
                              pso, pko = psos[hp]

                              def pv(e):
                                  ins = None
                                  for kb in range(2):
                                      ins = e.matmul(pso[hh * 64:(hh + 1) * 64, qb * 128:(qb + 1) * 128],
                                                     Vt[:, qb + kb, kv * 64:(kv + 1) * 64], PTs[:, u, i * 2 + kb, :],
                                                     start=(kb == 0), stop=(kb == 1))
                                  return ins
                              S.op("pe", pv, reads=[ptk, "Vt"], writes=[pko])

                      def attn_round(units):
                          order = [("a", 0), ("a", 1), ("b", 0), ("a", 2), ("b", 1), ("c", 0), ("a", 3), ("b", 2), ("c", 1), ("b", 3), ("c", 2), ("c", 3)]
                          fns = {"a": attn_stage_a, "b": attn_stage_b, "c": attn_stage_c}
                          for st_, n in order:
                              fns[st_](units[n])

                      def ssm_tail(cu):
                          ps_, pk_ = psy[cu]
                          S.op("dve", lambda e: e.scalar_tensor_tensor(ys_t[:, cu, :], uT32[:, cu, :], sm(l, "ssmd", cu, cu + 1), ps_, ALU.mult, ALU.add),
                               reads=[pk_, "uT32", "small"], writes=["cacc%d" % cu])
                          S.op("act", lambda e: e.activation(gT[:, cu, :], ys_t[:, cu, :], AF.Gelu_apprx_tanh), reads=["cacc%d" % cu], writes=["convact2"])
                          release(pk_)

                      gsched = ((0, 1), (2,), (3, 4), (5,))
                      for r in range(4):
                          kv = r // 2
                          if r % 2 == 0:
                              psos[2 * kv] = bank(hold=True)
                              psos[2 * kv + 1] = bank(hold=True)
                          ssm_front(r)
                          for g in gsched[r]:
                              gates_group(g)
                          units = []
                          for qb in (2 * (r % 2), 2 * (r % 2) + 1):
                              for hh in range(2):
                                  units.append((qb, kv, hh, len(units) % 2))
                          attn_round(units)
                          if r % 2 == 1:
                              for hp in (2 * kv, 2 * kv + 1):
                                  pso, pko = psos[hp]
                                  S.op("act", lambda e: e.copy(yattn[:, hp, :], pso), reads=[pko], writes=["yattn"])
                                  release(pko)
                          if r >= 1:
                              ssm_back(r - 1)
                              if (r - 1) % 2 == 1:
                                  ssm_tail((r - 1) // 2)
                          ssm_demod(r)
                      chk("attn")
                      S.op("pool", lambda e, l=l: e.tensor_copy(kcar[:, l, :, :], kT[:, :, 512:640]), reads=["kT"], writes=["kcar"])
                      S.op("pool", lambda e, l=l: e.tensor_copy(vcar[:, l, :], Vt[:, 4, :]), reads=["Vt"], writes=["vcar"])
                      for g in range(2):
                          wva, wka = wload(l, PIECES["wao"][g])
                          for i in range(4):
                              m = g * 4 + i
                              ps, pk = bank()
                              mm_chain(ps, [(wva[:, k, i * 128:(i + 1) * 128], yattn[:, k, :]) for k in range(4)], [wka, "yattn"], pk)
                              S.op("dve", lambda e: e.tensor_tensor(merged[:, m, :], ps, gates[:, m, :], ALU.mult), reads=[pk, "gates"], writes=["merged"])
                      for g in range(2):
                          wvc, wkc = wload(l, PIECES["wco"][g])
                          for i in range(4):
                              m = g * 4 + i
                              ps, pk = bank()
                              mm_chain(ps, [(wvc[:, k, i * 128:(i + 1) * 128], convact[:, k, :]) for k in range(2)], [wkc, "convact"], pk)
                              S.op("dve", lambda e: e.tensor_tensor(tmp[:, 0, :], ps, gates[:, 16 + m, :], ALU.mult), reads=[pk, "gates"], writes=["tmp0"])
                              S.op("dve", lambda e: e.tensor_tensor(merged[:, m, :], merged[:, m, :], tmp[:, 0, :], ALU.add), reads=["tmp0", "merged"], writes=["merged"])
                      chk("conv")
                      ssm_back(3)
                      ssm_tail(1)
                      for g in range(4):
                          wvg, wkg = wload(l, PIECES["wglu"][g])
                          for i in range(2):
                              m = g * 2 + i
                              psb_, pkb_ = bank()
                              mm_chain(psb_, [(wvg[:, k, (i + 2) * 128:(i + 3) * 128], gT[:, k, :]) for k in range(2)], [wkg, "convact2"], pkb_)
                              ta_, tb_ = 2 + (m % 2), m % 2
                              S.op("act", lambda e: e.activation(tmp[:, ta_, :], psb_, AF.Sigmoid, bias=sm(l, "bglu", 8 + m, 9 + m)), reads=[pkb_, "small"], writes=["tmp%d" % ta_])
                              psa_, pka_ = bank()
                              mm_chain(psa_, [(wvg[:, k, i * 128:(i + 1) * 128], gT[:, k, :]) for k in range(2)], [wkg, "convact2"], pka_)
                              S.op("dve", lambda e: e.scalar_tensor_tensor(tmp[:, tb_, :], psa_, sm(l, "bglu", m, m + 1), tmp[:, ta_, :], ALU.add, ALU.mult),
                                   reads=[pka_, "tmp%d" % ta_, "small"], writes=["tmp%d" % tb_])
                              S.op("dve", lambda e: e.tensor_tensor(tmp[:, tb_, :], tmp[:, tb_, :], gates[:, 8 + m, :], ALU.mult), reads=["tmp%d" % tb_, "gates"], writes=["tmp%d" % tb_])
                              S.op("dve", lambda e: e.tensor_tensor(merged[:, m, :], merged[:, m, :], tmp[:, tb_, :], ALU.add), reads=["tmp%d" % tb_, "merged"], writes=["merged"])
                      chk("ssm")
                      for c in range(8):
                          S.op("act", lambda e, c=c: e.copy(hT[:, c, :], merged[:, c, :]), reads=["merged"], writes=["hT"])
                      for g in range(2):
                          wvm, wkm = wload(l, PIECES["wmo"][g])
                          for i in range(4):
                              m = g * 4 + i
                              ps, pk = bank()
                              mm_chain(ps, [(wvm[:, k, i * 128:(i + 1) * 128], hT[:, k, :]) for k in range(8)], [wkm, "hT"], pk)
                              S.op("dve", lambda e, m=m, ps=ps: e.tensor_tensor(xT[:, m, :], ps, xT[:, m, :], ALU.add), reads=[pk, "xT"], writes=["xT"])
                              sq_chunk(m)

                      chk("mix")
                      rmsnorm_to_hT(sm(l, "ffng"))
                      for jg in range(11):
                          wvf, wkf = wload(l, PIECES["wfi"][jg])
                          for i in range(2):
                              hc = jg * 2 + i
                              psg, pkg = bank()
                              mm_chain(psg, [(wvf[:, k, i * 128:(i + 1) * 128], hT[:, k, :]) for k in range(8)], [wkf, "hT"], pkg)
                              ta_ = 2 + (hc % 2)
                              S.op("act", lambda e, psg=psg: e.activation(tmp[:, ta_, :], psg, AF.Silu), reads=[pkg], writes=["tmp%d" % ta_])
                              psu, pku = bank()
                              mm_chain(psu, [(wvf[:, k, (i + 2) * 128:(i + 3) * 128], hT[:, k, :]) for k in range(8)], [wkf, "hT"], pku)
                              S.op("dve", lambda e, hc=hc, psu=psu: e.tensor_tensor(act_ffn[:, hc, :], psu, tmp[:, ta_, :], ALU.mult),
                                   reads=[pku, "tmp%d" % ta_], writes=["gates"])
                      for ng in range(2):
                          pss_ = [bank(hold=True) for _ in range(4)]
                          k0 = 0
                          for pi_, piece in enumerate(PIECES["wfo"][ng * 3:(ng + 1) * 3]):
                              wvo, wko = wload(l, piece)
                              kcp = piece[1]
                              for i in range(4):
                                  ps, pk = pss_[i]

                                  def fo(e, ps=ps, wvo=wvo, kcp=kcp, k0=k0, i=i):
                                      ins = None
                                      for k in range(kcp):
                                          ins = e.matmul(ps, wvo[:, k, i * 128:(i + 1) * 128], act_ffn[:, k0 + k, :],
                                                         start=(k0 + k == 0), stop=(k0 + k == 21))
                                      return ins
                                  S.op("pe", fo, reads=[wko, "gates"], writes=[pk])
                              k0 += kcp
                          for i in range(4):
                              m = ng * 4 + i
                              ps, pk = pss_[i]
                              S.op("dve", lambda e, m=m, ps=ps: e.tensor_tensor(xT[:, m, :], ps, xT[:, m, :], ALU.add), reads=[pk, "xT"], writes=["xT"])
                              sq_chunk(m)
                              release(pk)

                      chk("ffn")
                      dump("pT", pT[:].rearrange("p c t -> p (c t)"), "pT", 1024, (s, j, l))
                      dump("ptok", ptok[:].rearrange("p a d -> p (a d)"), "ptok", 1024, (s, j, l))
                      rmsnorm_to_hT(sm(l, "pleg"))
                      for g in range(2):
                          wvi, wki = wload(l, PIECES["wpi"][g])
                          wvp, wkp = wload(l, PIECES["wpg"][g])
                          for i in range(4):
                              m = g * 4 + i
                              psg, pkg = bank()
                              mm_chain(psg, [(wvp[:, k, i * 128:(i + 1) * 128], hT[:, k, :]) for k in range(8)], [wkp, "hT"], pkg)
                              ta_, tb_ = 2 + (m % 2), m % 2
                              S.op("act", lambda e, psg=psg: e.activation(tmp[:, ta_, :], psg, AF.Sigmoid), reads=[pkg], writes=["tmp%d" % ta_])
                              pse, pke = bank()
                              mm_chain(pse, [(wvi[:, k, i * 128:(i + 1) * 128], pT[:, k, :]) for k in range(2)], [wki, "pT"], pke)
                              S.op("dve", lambda e, pse=pse: e.tensor_tensor(tmp[:, tb_, :], pse, tmp[:, ta_, :], ALU.mult), reads=[pke, "tmp%d" % ta_], writes=["tmp%d" % tb_])
                              S.op("dve", lambda e, m=m: e.tensor_tensor(xT[:, m, :], xT[:, m, :], tmp[:, tb_, :], ALU.add), reads=["tmp%d" % tb_, "xT"], writes=["xT"])
                              sq_chunk(m)

                      chk("ple_end")
                  except _Stop:
                    stopped = True
                    break
                dump("xT", xT[:].rearrange("p c t -> p (c t)"), "xT", 4096, (s, j))
                if final_norm:
                    rmsnorm_to_hT(small[:, G_FINALG:G_FINALG + 8], out_f32=merged)
                    src, srck = merged, "merged"
                else:
                    src, srck = xT, "xT"
                ost = [w4[:].rearrange("p a b t -> p (a b t)"), r4[:].rearrange("p a b t -> p (a b t)")]
                for tb in range(4):
                    stg = ost[tb % 2]
                    stk_ = "w4" if tb % 2 == 0 else "r4"
                    for half in range(2):
                        ps, pk = bank()

                        def tro(e, ps=ps, tb=tb, half=half, src=src):
                            ins = None
                            for cc in range(4):
                                c = half * 4 + cc
                                ins = e.transpose(ps[:, cc * 128:(cc + 1) * 128], src[:, c, tb * 128:(tb + 1) * 128], identf)
                            return ins
                        S.op("pe", tro, reads=[srck, "cst"], writes=[pk])
                        copy_op(evac_engine(), stg[:, half * 512:(half + 1) * 512], ps, [pk], [stk_])
                    S.dma("sp", lambda e, s=s, t0=t0, tb=tb, stg=stg: e.dma_start(out=out_d[s, t0 + tb * 128:t0 + (tb + 1) * 128, :], in_=stg[:, 0:1024]),
                          "out", reads=[stk_], writes=["out"])
                if stopped:
                    break
            if stopped:
                break
        S.final_wait("sp", ["out"] + ["dbg_" + n for n in DBG])
        S.emit(st)
    return nc


_CACHE = {}


def kernel(**inputs):
    x = np.ascontiguousarray(np.asarray(inputs["x"], np.float32))
    p = np.asarray(inputs["p"], np.float32)
    pos = np.ascontiguousarray(np.asarray(inputs["positions"], np.int32))
    wf, craw, small, cst = host_prep(inputs)
    nb = x.shape[0] // NCORE
    nc = build(nb, x.shape[1] // T, list(range(DEPTH)))
    in_maps = []
    for c in range(NCORE):
        sl = slice(c * nb, (c + 1) * nb)
        in_maps.append({"x": x[sl], "p": np.ascontiguousarray(p[:, sl]), "pos": pos[sl], "wf": wf, "craw": craw,
                        "small": small, "cst": cst})
    res = run_bass_kernel_spmd(nc, in_maps, core_ids=list(range(NCORE)))
    return np.concatenate([r["out"] for r in res.results], axis=0).astype(np.float32)
```

```python
import math
from contextlib import ExitStack

import numpy as np
import concourse.bass as bass
import concourse.mybir as mybir
from concourse.bass_utils import run_bass_kernel_spmd

F32 = mybir.dt.float32
BF16 = mybir.dt.bfloat16
I32 = mybir.dt.int32
AF = mybir.ActivationFunctionType
ALU = mybir.AluOpType
AX = mybir.AxisListType

D = 1024
T = 512
DEPTH = 4
NCORE = 8
SEQ = 4096
EPS = 1e-6
TWO_PI = 2.0 * math.pi
MAGIC = 12582912.0
NEG = -30000.0
NBUF = 3
WSLOT = 4096


class Sched:
    ENG = ("pe", "act", "dve", "pool", "sp")

    def __init__(self, nc):
        self.nc = nc
        self.ops = {e: [] for e in self.ENG}
        self.cnt = {}
        self.seen = {e: {} for e in self.ENG}
        self.lastw = {}
        self.readers = {}

    def _deps(self, engine, reads, writes):
        waits = {}

        def need(k, v):
            if k == engine and engine == "pe":
                return
            if self.seen[engine].get(k, 0) >= v:
                return
            if waits.get(k, 0) < v:
                waits[k] = v

        for r in reads:
            lw = self.lastw.get(r)
            if lw is not None:
                need(*lw)
        for w in writes:
            lw = self.lastw.get(w)
            if lw is not None:
                need(*lw)
            for k, v in self.readers.get(w, {}).items():
                need(k, v)
        for k, v in waits.items():
            self.seen[engine][k] = v
        return waits

    def _commit(self, key, val, reads, writes):
        for r in reads:
            self.readers.setdefault(r, {})[key] = val
        for w in writes:
            self.lastw[w] = (key, val)
            self.readers[w] = {}

    @staticmethod
    def _record(fn):
        calls = []

        class _P:
            def __getattr__(self, name):
                def f(*a, **k):
                    calls.append((name, a, k))
                    return self
                return f
        fn(_P())
        assert calls, "op emitted no instruction"

        def replay(eng):
            ins = None
            for name, a, k in calls:
                ins = getattr(eng, name)(*a, **k)
            return ins
        return replay

    def op(self, engine, fn, reads=(), writes=()):
        fn = self._record(fn)
        px = [r for r in reads if r.startswith("ps")]
        if px:
            writes = list(writes) + [r for r in px if r not in writes]
        waits = self._deps(engine, reads, writes)
        val = self.cnt.get(engine, 0) + 1
        self.cnt[engine] = val
        self.ops[engine].append((waits, fn, engine, 1))
        self._commit(engine, val, reads, writes)

    def dma(self, queue, fn, slot, reads=(), writes=()):
        fn = self._record(fn)
        key = "dma:" + slot
        waits = self._deps(queue, reads, writes)
        val = self.cnt.get(key, 0) + 16
        self.cnt[key] = val
        self.ops[queue].append((waits, fn, key, 16))
        self._commit(key, val, reads, writes)

    def final_wait(self, queue, slots):
        waits = {}
        for s in slots:
            key = "dma:" + s
            if key in self.cnt:
                waits[key] = self.cnt[key]
        self.ops[queue].append((waits, None, None, 0))

    def emit(self, stack):
        nc = self.nc
        sems = {}
        for i, k in enumerate(list(self.cnt)):
            sems[k] = stack.enter_context(nc.semaphore("s%d_%s" % (i, k.replace(":", "_"))))
        block = stack.enter_context(nc.Block())
        ops = self.ops

        def run(eng, lst):
            for waits, fn, key, amt in lst:
                for k, v in waits.items():
                    eng.wait_ge(sems[k], v)
                if fn is not None:
                    fn(eng).then_inc(sems[key], amt)

        @block.tensor
        def _(e):
            run(e, ops["pe"])

        @block.scalar
        def _(e):
            run(e, ops["act"])

        @block.vector
        def _(e):
            run(e, ops["dve"])

        @block.gpsimd
        def _(e):
            run(e, ops["pool"])

        @block.sync
        def _(e):
            run(e, ops["sp"])


def _pack(w):
    kcp = w.shape[0] // 128
    return np.ascontiguousarray(w.reshape(kcp, 128, -1).transpose(1, 0, 2)).reshape(-1)


def _partner(r):
    if r < 8:
        return r + 8
    if r < 16:
        return r - 8
    return r


_PM = np.array([_partner(r) for r in range(64)])

PIECES = {}
LW = 0


def _mk_pieces():
    global LW
    off = 0

    def add(name, kcp, ncols, n):
        nonlocal off
        lst = []
        for _ in range(n):
            lst.append((off, kcp, ncols))
            off += 128 * kcp * ncols
        PIECES[name] = lst

    add("win", 8, 512, 11)
    add("wao", 4, 512, 2)
    add("wco", 2, 512, 2)
    add("bpad", 16, 128, 1)
    add("wglu", 2, 512, 4)
    add("wmo", 8, 512, 2)
    add("wfi", 8, 512, 11)
    PIECES["wfo"] = []
    for ng in range(2):
        for kcp in (8, 8, 6):
            PIECES["wfo"].append((off, kcp, 512))
            off += 128 * kcp * 512
    add("wpi", 2, 512, 2)
    add("wpg", 8, 512, 2)
    LW = off


_mk_pieces()

SMALL_FIELDS = [("mixg", 8), ("ffng", 8), ("pleg", 8), ("bgate", 24), ("bglu", 16), ("convw", 62),
                ("convb", 2), ("lng", 2), ("lnb", 2), ("ssmd", 2), ("sinks", 8), ("lamre", 8),
                ("lamim", 8), ("logdt", 8)]
SOFF = {}
_o = 0
for _n, _w in SMALL_FIELDS:
    SOFF[_n] = (_o, _w)
    _o += _w
LS = _o
G_FINALG = DEPTH * LS
G_INVF = G_FINALG + 8
G_SGN = G_INVF + 1
NS = G_SGN + 1
C_ID, C_MN, C_MF, C_TV, NCST = 0, 128, 384, 640, 768


def _chunked(v):
    return np.ascontiguousarray(v.reshape(-1, 128).T)


def host_prep(inp):
    f = np.float32
    wf = np.zeros((DEPTH, LW), f)
    craw = np.zeros((DEPTH, 128, 2, 8, 128), f)
    small = np.zeros((128, NS), f)
    for l in range(DEPTH):
        w_in = np.asarray(inp["w_in"][l], f)
        cq = np.arange(512)
        cqp = np.concatenate([h * 64 + _PM for h in range(8)])
        ck = np.concatenate([512 + g * 64 + np.arange(64) for g in (0, 0, 1, 1)])
        ckp = np.concatenate([512 + g * 64 + _PM for g in (0, 0, 1, 1)])
        cv = 640 + np.arange(128)
        cs = 768 + np.arange(256)
        ca = 1024 + np.arange(512)
        cg = 1536 + np.arange(3072)
        cols = np.concatenate([cq, cqp, ck, ckp, cs, cv, cv, ca, cg])
        assert cols.size == 5632
        w1 = w_in[:, cols]
        parts = []
        for g in range(11):
            parts.append(_pack(w1[:, g * 512:(g + 1) * 512]))
        wao = np.asarray(inp["w_attn_out"][l], f)
        for g in range(2):
            parts.append(_pack(wao[:, g * 512:(g + 1) * 512]))
        wco = np.asarray(inp["w_conv_out"][l], f)
        for g in range(2):
            parts.append(_pack(wco[:, g * 512:(g + 1) * 512]))
        bp = np.zeros((128, 16, 128), f)
        bre = np.asarray(inp["ssm_b_re"][l], f)
        bim = np.asarray(inp["ssm_b_im"][l], f)
        cre = np.asarray(inp["ssm_c_re"][l], f)
        cim = np.asarray(inp["ssm_c_im"][l], f)
        for gp in range(8):
            for g2 in range(2):
                g = 2 * gp + g2
                r0 = (gp % 4) * 32 + g2 * 16
                bp[r0:r0 + 16, gp * 2 + 0, g2 * 64:(g2 + 1) * 64] = bre[g].T
                bp[r0:r0 + 16, gp * 2 + 1, g2 * 64:(g2 + 1) * 64] = bim[g].T
                craw[l, g2 * 64:(g2 + 1) * 64, 0, gp, r0:r0 + 16] = cre[g].T
                craw[l, g2 * 64:(g2 + 1) * 64, 1, gp, r0:r0 + 16] = cim[g].T
        parts.append(bp.reshape(-1))
        wg = np.asarray(inp["w_ssm_glu"][l], f)
        for j in range(4):
            cc = np.concatenate([j * 256 + np.arange(256), 1024 + j * 256 + np.arange(256)])
            parts.append(_pack(wg[:, cc]))
        wmo = np.asarray(inp["w_mix_out"][l], f)
        for g in range(2):
            parts.append(_pack(wmo[:, g * 512:(g + 1) * 512]))
        wfi = np.asarray(inp["w_ffn_in"][l], f)
        for j in range(11):
            cc = np.concatenate([j * 256 + np.arange(256), 2816 + j * 256 + np.arange(256)])
            parts.append(_pack(wfi[:, cc]))
        wfo = np.asarray(inp["w_ffn_out"][l], f)
        for ng in range(2):
            for (k0, k1) in ((0, 8), (8, 16), (16, 22)):
                parts.append(_pack(wfo[k0 * 128:k1 * 128, ng * 512:(ng + 1) * 512]))
        wpi = np.asarray(inp["w_ple_in"][l], f)
        for g in range(2):
            parts.append(_pack(wpi[:, g * 512:(g + 1) * 512]))
        wpg = np.asarray(inp["w_ple_gate"][l], f)
        for g in range(2):
            parts.append(_pack(wpg[:, g * 512:(g + 1) * 512]))
        flat = np.concatenate(parts)
        assert flat.size == LW, (flat.size, LW)
        wf[l] = flat
        b = l * LS

        def put(name, arr):
            o, w = SOFF[name]
            assert arr.shape == (128, w), (name, arr.shape)
            small[:, b + o:b + o + w] = arr

        put("mixg", _chunked(np.asarray(inp["mix_norm_g"][l], f)))
        put("ffng", _chunked(np.asarray(inp["ffn_norm_g"][l], f)))
        put("pleg", _chunked(np.asarray(inp["ple_norm_g"][l], f)))
        put("bgate", _chunked(np.asarray(inp["b_gate"][l], f)))
        put("bglu", _chunked(np.asarray(inp["b_ssm_glu"][l], f)))
        cw = np.asarray(inp["conv_dw_w"][l], f)
        cwl = np.zeros((128, 62), f)
        for c in range(2):
            cwl[:, c * 31:(c + 1) * 31] = cw[:, c * 128:(c + 1) * 128].T
        put("convw", cwl)
        put("convb", _chunked(np.asarray(inp["conv_dw_b"][l], f)))
        put("lng", _chunked(np.asarray(inp["conv_norm_g"][l], f)))
        put("lnb", _chunked(np.asarray(inp["conv_norm_b"][l], f)))
        put("ssmd", _chunked(np.asarray(inp["ssm_d"][l], f)))
        put("sinks", np.broadcast_to(np.asarray(inp["attn_sinks"][l], f)[None, :], (128, 8)))
        lre = np.asarray(inp["ssm_lambda_re"][l], f)
        lim = np.asarray(inp["ssm_lambda_im"][l], f)
        ldt = np.asarray(inp["ssm_log_dt"][l], f)
        a1 = np.zeros((128, 8), f)
        a2 = np.zeros((128, 8), f)
        a3 = np.zeros((128, 8), f)
        for gp in range(8):
            for g2 in range(2):
                a1[g2 * 64:(g2 + 1) * 64, gp] = lre[2 * gp + g2]
                a2[g2 * 64:(g2 + 1) * 64, gp] = lim[2 * gp + g2]
                a3[g2 * 64:(g2 + 1) * 64, gp] = ldt[2 * gp + g2]
        put("lamre", a1)
        put("lamim", a2)
        put("logdt", a3)
    small[:, G_FINALG:G_FINALG + 8] = _chunked(np.asarray(inp["final_norm_g"], f))
    inv_freq = (np.float32(500000.0) ** (-np.arange(0, 16, 2, dtype=np.float32) / np.float32(16))).astype(f)
    for p in range(128):
        r = p % 64
        if r < 8:
            small[p, G_INVF] = inv_freq[r]
            small[p, G_SGN] = -1.0
        elif r < 16:
            small[p, G_INVF] = inv_freq[r - 8]
            small[p, G_SGN] = 1.0
    cst = np.zeros((128, NCST), f)
    cst[:, C_ID:C_ID + 128] = np.eye(128, dtype=f)
    qi = np.arange(128)[:, None]
    kj = np.arange(256)[None, :]
    dist = qi + 128 - kj
    ok = (dist >= 0) & (dist < 128)
    cst[:, C_MN:C_MN + 256] = np.where(ok, 0.0, NEG)
    cst[:, C_MF:C_MF + 256] = np.where(ok & (kj >= 128), 0.0, NEG)
    cst[:, C_TV:C_TV + 128] = np.arange(1, 129, dtype=f)[None, :]
    return wf, craw.reshape(DEPTH, 128, 2048), small, cst


class _Stop(Exception):
    pass


STOP = None
SKIP = set()
DBG = {}


_CNT = {}


def chk(name):
    if STOP is None:
        return
    nm, _, n = STOP.partition("@")
    if nm != name:
        return
    c = _CNT.get(name, 0)
    _CNT[name] = c + 1
    if c == int(n or 0):
        raise _Stop()


def build(n_seq, n_tiles, layers, final_norm=True, dbg=False):
    nc = bass.Bass("TRN2", target_bir_lowering=False)
    ntok = n_tiles * T
    x_d = nc.dram_tensor("x", [n_seq, ntok, D], F32, kind="ExternalInput").ap()
    p_d = nc.dram_tensor("p", [DEPTH, n_seq, ntok, 256], F32, kind="ExternalInput").ap()
    pos_d = nc.dram_tensor("pos", [n_seq, ntok], I32, kind="ExternalInput").ap()
    wf_d = nc.dram_tensor("wf", [DEPTH, LW], F32, kind="ExternalInput").ap()
    craw_d = nc.dram_tensor("craw", [DEPTH, 128, 2048], F32, kind="ExternalInput").ap()
    small_d = nc.dram_tensor("small", [128, NS], F32, kind="ExternalInput").ap()
    cst_d = nc.dram_tensor("cst", [128, NCST], F32, kind="ExternalInput").ap()
    out_d = nc.dram_tensor("out", [n_seq, ntok, D], F32, kind="ExternalOutput").ap()
    dbg_d = nc.dram_tensor("dbg", [128, 8192], F32, kind="ExternalOutput").ap() if DBG else None
    wscr = nc.dram_tensor("wscr", [DEPTH, LW], BF16, kind="Internal").ap()
    cscr = nc.dram_tensor("cscr", [DEPTH, 128, 3072], BF16, kind="Internal").ap()
    tscr = nc.dram_tensor("tscr", [DEPTH, 128, 2048], F32, kind="Internal").ap()
    dscr = nc.dram_tensor("dscr", [DEPTH, 2, 128, 31 * 128], BF16, kind="Internal").ap()

    with ExitStack() as st:
        def sb(name, shape, dt):
            return st.enter_context(nc.sbuf_tensor("sb_" + name, shape, dt))

        xT = sb("xT", [128, 8, T], F32)
        hT = sb("hT", [128, 8, T], BF16)
        rstd = sb("rstd", [128, T], F32)
        wbuf = sb("wbuf", [128, NBUF, WSLOT], BF16)
        gates = sb("gates", [128, 24, T], BF16)
        merged = sb("merged", [128, 8, T], F32)
        cosF = sb("cosF", [128, T], F32)
        sinF = sb("sinF", [128, T], F32)
        qtmp = sb("qtmp", [128, 2, T], F32)
        posi_t = sb("posi", [128, T], I32)
        qT = sb("qT", [128, 4, T], BF16)
        kT = sb("kT", [128, 2, 640], BF16)
        Vt = sb("Vt", [128, 5, 128], BF16)
        kcar = sb("kcar", [128, DEPTH, 2, 128], BF16)
        vcar = sb("vcar", [128, DEPTH, 128], BF16)
        Sm = sb("Sm", [128, 2, 2, 256], F32)
        Pb = sb("Pb", [128, 2, 2, 256], BF16)
        PTs = sb("PTs", [128, 2, 4, 128], BF16)
        stat = sb("stat", [128, 2, 16], F32)
        yattn = sb("yattn", [128, 4, T], BF16)
        uT32 = sb("uT32", [128, 2, T], F32)
        uTb = sb("uTb", [128, 2, T], BF16)
        ucb = sb("ucb", [128, 2, 30 + T], BF16)
        ucar = sb("ucar", [128, DEPTH, 2, 30], BF16)
        bpad_t = sb("bpad_t", [128, 16, 128], BF16)
        cpad_t = sb("cpad_t", [128, 24, 128], BF16)
        gT2 = sb("gT2", [128, 2, T], BF16)
        tmp = sb("tmp", [128, 4, T], F32)
        cacc = sb("cacc", [128, 2, T], F32)
        convact = sb("convact", [128, 2, T], BF16)
        w4 = sb("w4", [128, 2, 2, T], F32)
        r4 = sb("r4", [128, 2, 2, T], F32)
        Pp = sb("Pp", [128, 2, 4, T], BF16)
        tab = sb("tab", [128, 2, 8, 128], F32)
        scar = sb("scar", [128, DEPTH, 2, 8], F32)
        rho = sb("rho", [128, DEPTH, 8], F32)
        s8 = sb("s8", [128, 24, 8], F32)
        ptok = sb("ptok", [128, 4, 256], F32)
        ptokb = sb("ptokb", [128, 4, 256], BF16)
        pT = sb("pT", [128, 2, T], BF16)
        small = sb("small", [128, NS], F32)
        cst = sb("cst", [128, NCST], F32)
        identb = sb("identb", [128, 128], BF16)
        maskb = sb("maskb", [128, 2, 2, 256], BF16)
        onesb = sb("onesb", [128, 128], BF16)
        onesf = sb("onesf", [128, 128], F32)
        psf = st.enter_context(nc.psum_tensor("psf", [128, 6, T], F32))
        psb = st.enter_context(nc.psum_tensor("psb", [128, 2, 8, 128], BF16))

        act_ffn = gates[:, 0:22, :]
        ys = cacc
        ys_t = cacc
        gT = gT2
        xio = merged[:].rearrange("p c t -> p (c t)").rearrange("p (tb d) -> p tb d", tb=4)
        identf = cst[:, C_ID:C_ID + 128]
        maskN = cst[:, C_MN:C_MN + 256]
        maskF = cst[:, C_MF:C_MF + 256]
        tvec = cst[:, C_TV:C_TV + 128]

        S = Sched(nc)
        state = {"bank": 0, "wslot": 0, "alt": 0}

        held = set()

        def bank(hold=False):
            while True:
                b = state["bank"] % 6
                state["bank"] += 1
                if b not in held:
                    break
            if hold:
                held.add(b)
            return psf[:, b, :], "ps%d" % b

        def release(pk):
            held.discard(int(pk[2:]))

        def sm(l, name, c0=0, c1=None):
            o, w = SOFF[name]
            if c1 is None:
                c1 = w
            return small[:, l * LS + o + c0:l * LS + o + c1]

        def evac_engine():
            state["alt"] ^= 1
            return "act" if state["alt"] else "dve"

        def copy_op(eng, out, in_, reads, writes):
            if eng == "act":
                S.op("act", lambda e: e.copy(out, in_), reads=reads, writes=writes)
            else:
                S.op(eng, lambda e: e.tensor_copy(out, in_), reads=reads, writes=writes)

        def wload(l, piece, src=None, srckey=None):
            off, kcp, ncols = piece
            slot = state["wslot"] % NBUF
            state["wslot"] += 1
            n = kcp * ncols
            view = wbuf[:, slot, 0:n]
            if src is None:
                src = wscr[l, off:off + 128 * n].rearrange("(p f) -> p f", p=128)
                srckey = "wscr%d" % l
            S.dma("sp", lambda e: e.dma_start(out=view, in_=src), "w%d" % slot,
                  reads=[srckey], writes=["w%d" % slot])
            return view.rearrange("p (k n) -> p k n", k=kcp), "w%d" % slot

        def mm_chain(out, pairs, reads, wkey):
            def fn(e):
                n = len(pairs)
                ins = None
                for i, (a, b) in enumerate(pairs):
                    ins = e.matmul(out, a, b, start=(i == 0), stop=(i == n - 1))
                return ins
            S.op("pe", fn, reads=reads, writes=[wkey])


        def early_exit():
            S.op("dve", lambda e: e.memset(w4[:], 1.0), writes=["w4"])
            S.dma("sp", lambda e: e.dma_start(out=out_d[0, 0:128, :], in_=w4[:].rearrange("p a b t -> p (a b t)")[:, 0:1024]),
                  "out", reads=["w4"], writes=["out"])
            S.final_wait("sp", ["out", "small", "cst"] + ["pp%d" % l for l in layers] + ["tscr%d" % l for l in layers] + ["cscr%d" % l for l in layers])
            S.emit(st)
            return nc
        def dump(name, ap, key, ncols, tile_sel=None):
            if name not in DBG:
                return
            c0, tsel = DBG[name]
            if tsel is not None and tsel != tile_sel:
                return
            S.dma("pool", lambda e: e.dma_start(out=dbg_d[:, c0:c0 + ncols], in_=ap), "dbg_" + name, reads=[key], writes=["dbg"])

        S.dma("sp", lambda e: e.dma_start(out=small[:], in_=small_d[:, :]), "small", writes=["small"])
        S.dma("sp", lambda e: e.dma_start(out=cst[:], in_=cst_d[:, :]), "cst", writes=["cst"])
        CH = 128 * 16384
        for l in layers:
            o = 0
            while o < LW:
                n = min(CH, LW - o)
                S.dma("pool", lambda e, l=l, o=o, n=n: e.dma_start(
                    out=wscr[l, o:o + n].rearrange("(p f) -> p f", p=128),
                    in_=wf_d[l, o:o + n].rearrange("(p f) -> p f", p=128)),
                    "pp%d" % l, writes=["wscr%d" % l])
                o += n
        if STOP == "setup0":
            return early_exit()
        S.op("dve", lambda e: e.tensor_copy(identb[:], identf), reads=["cst"], writes=["identb"])
        S.op("dve", lambda e: e.memset(onesb[:], 1.0), writes=["onesb"])
        for h_ in range(2):
            S.op("dve", lambda e, h_=h_: e.tensor_copy(maskb[:, 0, h_, :], maskN), reads=["cst"], writes=["maskb"])
            S.op("dve", lambda e, h_=h_: e.tensor_copy(maskb[:, 1, h_, :], maskF), reads=["cst"], writes=["maskb"])
        S.op("dve", lambda e: e.memset(onesf[:], 1.0 / 256.0), writes=["onesf"])
        S.op("dve", lambda e: e.memset(kcar[:], 0.0), writes=["kcar"])
        S.op("dve", lambda e: e.memset(vcar[:], 0.0), writes=["vcar"])
        S.op("dve", lambda e: e.memset(kT[:], 0.0), writes=["kT"])
        S.op("dve", lambda e: e.memset(Vt[:], 0.0), writes=["Vt"])

        def sin_of(out_ap, a_ap, shift, tv, tk, rk, wk, tvk, tkk):
            S.op("dve", lambda e: e.tensor_scalar(tv, a_ap, 1.0 / TWO_PI, shift / TWO_PI, ALU.mult, ALU.add),
                 reads=list(rk), writes=[tvk])
            S.op("dve", lambda e: e.tensor_scalar(tk, tv, MAGIC, None, ALU.add), reads=[tvk], writes=[tkk])
            S.op("dve", lambda e: e.tensor_scalar(tk, tk, MAGIC, None, ALU.subtract), reads=[tkk], writes=[tkk])
            S.op("dve", lambda e: e.tensor_tensor(tv, tv, tk, ALU.subtract), reads=[tvk, tkk], writes=[tvk])
            S.op("dve", lambda e: e.tensor_scalar(tv, tv, 0.49999, -0.49999, ALU.min, ALU.max), reads=[tvk], writes=[tvk])
            S.op("act", lambda e: e.activation(out_ap, tv, AF.Sin, scale=TWO_PI), reads=[tvk], writes=list(wk))

        w4f = w4[:].rearrange("p a b t -> p (a b t)")
        r4f = r4[:].rearrange("p a b t -> p (a b t)")
        tabf = tab[:].rearrange("p a g t -> p (a g t)")
        for l in layers:
            def s8v(i):
                return s8[:, i, :]
            lamre, lamim, logdt = sm(l, "lamre"), sm(l, "lamim"), sm(l, "logdt")
            DT, LR, TH, SN, CS, ARE, AIM, XRE, T1, T2, RD, FRE, FIM, NFRE, NFIM, TV, TK = [s8v(i) for i in range(17)]
            S.op("act", lambda e: e.activation(DT, logdt, AF.Exp), reads=["small"], writes=["s8"])
            S.op("dve", lambda e: e.tensor_scalar(LR, lamre, -1e-4, None, ALU.min), reads=["small"], writes=["s8"])
            S.op("dve", lambda e: e.tensor_tensor(T1, LR, DT, ALU.mult), reads=["s8"], writes=["s8"])
            S.op("act", lambda e, l=l: e.activation(rho[:, l, :], T1, AF.Exp), reads=["s8"], writes=["rho"])
            S.op("dve", lambda e: e.tensor_tensor(TH, lamim, DT, ALU.mult), reads=["s8", "small"], writes=["s8"])
            sin_of(SN, TH, 0.0, TV, TK, ["s8"], ["s8"], "s8", "s8")
            sin_of(CS, TH, math.pi / 2, TV, TK, ["s8"], ["s8"], "s8", "s8")
            S.op("dve", lambda e, l=l: e.tensor_tensor(ARE, rho[:, l, :], CS, ALU.mult), reads=["s8", "rho"], writes=["s8"])
            S.op("dve", lambda e, l=l: e.tensor_tensor(AIM, rho[:, l, :], SN, ALU.mult), reads=["s8", "rho"], writes=["s8"])
            S.op("dve", lambda e: e.tensor_scalar(XRE, ARE, -1.0, None, ALU.add), reads=["s8"], writes=["s8"])
            S.op("dve", lambda e: e.tensor_tensor(T1, LR, LR, ALU.mult), reads=["s8"], writes=["s8"])
            S.op("dve", lambda e: e.tensor_tensor(T2, lamim, lamim, ALU.mult), reads=["s8", "small"], writes=["s8"])
            S.op("dve", lambda e: e.tensor_tensor(T1, T1, T2, ALU.add), reads=["s8"], writes=["s8"])
            S.op("dve", lambda e: e.reciprocal(RD, T1), reads=["s8"], writes=["s8"])
            S.op("dve", lambda e: e.tensor_tensor(T1, XRE, LR, ALU.mult), reads=["s8"], writes=["s8"])
            S.op("dve", lambda e: e.tensor_tensor(T2, AIM, lamim, ALU.mult), reads=["s8", "small"], writes=["s8"])
            S.op("dve", lambda e: e.tensor_tensor(T1, T1, T2, ALU.add), reads=["s8"], writes=["s8"])
            S.op("dve", lambda e: e.tensor_tensor(FRE, T1, RD, ALU.mult), reads=["s8"], writes=["s8"])
            S.op("dve", lambda e: e.tensor_tensor(T1, AIM, LR, ALU.mult), reads=["s8"], writes=["s8"])
            S.op("dve", lambda e: e.tensor_tensor(T2, XRE, lamim, ALU.mult), reads=["s8", "small"], writes=["s8"])
            S.op("dve", lambda e: e.tensor_tensor(T1, T1, T2, ALU.subtract), reads=["s8"], writes=["s8"])
            S.op("dve", lambda e: e.tensor_tensor(FIM, T1, RD, ALU.mult), reads=["s8"], writes=["s8"])
            S.op("dve", lambda e: e.tensor_scalar(NFRE, FRE, -1.0, None, ALU.mult), reads=["s8"], writes=["s8"])
            S.op("dve", lambda e: e.tensor_scalar(NFIM, FIM, -1.0, None, ALU.mult), reads=["s8"], writes=["s8"])
            for gp in range(8):
                S.op("dve", lambda e, gp=gp: e.tensor_scalar(w4f[:, gp * 128:(gp + 1) * 128], tvec, TH[:, gp:gp + 1], None, ALU.mult),
                     reads=["s8", "cst"], writes=["w4"])
            sin_of(tabf[:, 1024:2048], w4f[:, 0:1024], 0.0, w4f[:, 1024:2048], r4f[:, 0:1024], ["w4"], ["tab"], "w4", "r4")
            sin_of(tabf[:, 0:1024], w4f[:, 0:1024], math.pi / 2, w4f[:, 1024:2048], r4f[:, 0:1024], ["w4"], ["tab"], "w4", "r4")
            S.dma("sp", lambda e, l=l: e.dma_start(out=tscr[l, :, :], in_=tabf), "tscr%d" % l, reads=["tab"], writes=["tscr%d" % l])
            crawt = merged[:].rearrange("p c t -> p (c t)")[:, 0:2048].rearrange("p (r g n) -> p r g n", r=2, g=8)
            cpad = hT[:].rearrange("p c t -> p (c t)")[:, 0:3072].rearrange("p (v n) -> p v n", n=128)
            tq = qtmp[:, 0, 0:128]
            S.dma("sp", lambda e, l=l: e.dma_start(out=merged[:].rearrange("p c t -> p (c t)")[:, 0:2048], in_=craw_d[l, :, :]),
                  "craw", writes=["merged"])
            for gp in range(8):
                cre_, cim_ = crawt[:, 0, gp, :], crawt[:, 1, gp, :]
                S.op("dve", lambda e, gp=gp, cim_=cim_: e.tensor_scalar(tq, cim_, FIM[:, gp:gp + 1], None, ALU.mult),
                     reads=["merged", "s8"], writes=["qtmp"])
                S.op("dve", lambda e, gp=gp, cre_=cre_: e.scalar_tensor_tensor(cpad[:, gp * 3 + 0, :], cre_, FRE[:, gp:gp + 1], tq, ALU.mult, ALU.subtract),
                     reads=["merged", "s8", "qtmp"], writes=["hT"])
                S.op("dve", lambda e, gp=gp, cre_=cre_: e.scalar_tensor_tensor(cpad[:, gp * 3 + 1, :], cre_, NFRE[:, gp:gp + 1], tq, ALU.mult, ALU.add),
                     reads=["merged", "s8", "qtmp"], writes=["hT"])
                S.op("dve", lambda e, gp=gp, cim_=cim_: e.tensor_scalar(tq, cim_, NFRE[:, gp:gp + 1], None, ALU.mult),
                     reads=["merged", "s8", "hT"], writes=["qtmp"])
                S.op("dve", lambda e, gp=gp, cre_=cre_: e.scalar_tensor_tensor(cpad[:, gp * 3 + 2, :], cre_, NFIM[:, gp:gp + 1], tq, ALU.mult, ALU.add),
                     reads=["merged", "s8", "qtmp"], writes=["hT"])
            S.dma("sp", lambda e, l=l: e.dma_start(out=cscr[l, :, :], in_=hT[:].rearrange("p c t -> p (c t)")[:, 0:3072]),
                  "cscr%d" % l, reads=["hT"], writes=["cscr%d" % l])
            dstage = gates[:].rearrange("p c t -> p (c t)")[:, 0:62 * 128].rearrange("p (k n) -> p k n", n=128)
            for k in range(62):
                S.op("dve", lambda e, k=k, l=l: e.tensor_scalar(dstage[:, k, :], identb[:], sm(l, "convw", k, k + 1), None, ALU.mult),
                     reads=["identb", "small"], writes=["gates"])
            for c in range(2):
                S.dma("sp", lambda e, l=l, c=c: e.dma_start(out=dscr[l, c, :, :], in_=gates[:].rearrange("p c t -> p (c t)")[:, c * 3968:(c + 1) * 3968]),
                      "dscr%d" % l, reads=["gates"], writes=["dscr%d" % l])

        if STOP == "setup1":
            return early_exit()
        def sqbuf(m):
            return (qT[:, m, :], "qT") if m < 4 else (yattn[:, m - 4, :], "yattn")

        def sq_chunk(m):
            dst, dk = sqbuf(m)
            S.op("act", lambda e: e.activation(dst, xT[:, m, :], AF.Square), reads=["xT"], writes=[dk])

        def rmsnorm_to_hT(gain_cols, out_f32=None):
            ps, pk = bank()
            mm_chain(ps, [(onesb[:], sqbuf(c)[0]) for c in range(8)], ["onesb", "qT", "yattn"], pk)
            S.op("act", lambda e: e.activation(rstd[:], ps, AF.Sqrt, scale=1.0 / D, bias=EPS), reads=[pk], writes=["rstd"])
            S.op("dve", lambda e: e.reciprocal(rstd[:], rstd[:]), reads=["rstd"], writes=["rstd"])
            for c in range(8):
                dst = hT[:, c, :] if out_f32 is None else out_f32[:, c, :]
                S.op("dve", lambda e, c=c, dst=dst: e.scalar_tensor_tensor(dst, xT[:, c, :], gain_cols[:, c:c + 1], rstd[:], ALU.mult, ALU.mult),
                     reads=["xT", "rstd", "small"], writes=["hT" if out_f32 is None else "merged"])

        def proj4(wv, wk, rhs_tile, rhs_key, kcn, chunks, consumer):
            for i in chunks:
                ps, pk = bank()
                mm_chain(ps, [(wv[:, k, i * 128:(i + 1) * 128], rhs_tile[:, k, :]) for k in range(kcn)], [wk, rhs_key], pk)
                consumer(i, ps, pk)

        for s in range(n_seq):
            for j in range(n_tiles):
                t0 = j * T
                first = (j == 0)
                S.dma("sp", lambda e, s=s, t0=t0: e.dma_start(out=xio, in_=x_d[s, t0:t0 + T, :].rearrange("(tb p) d -> p tb d", p=128)),
                      "xio", writes=["merged"])
                for c in range(8):
                    ps, pk = bank()

                    def tr(e, c=c, ps=ps):
                        ins = None
                        for tb in range(4):
                            ins = e.transpose(ps[:, tb * 128:(tb + 1) * 128], xio[:, tb, c * 128:(c + 1) * 128], identf)
                        return ins
                    S.op("pe", tr, reads=["merged", "cst"], writes=[pk])
                    copy_op(evac_engine(), xT[:, c, :], ps, [pk], ["xT"])
                    sq_chunk(c)
                t1a = (STOP == "t1a" and j == 1)
                if "rope" not in SKIP and not t1a:
                    posi = posi_t[:]
                    S.dma("sp", lambda e, s=s, t0=t0: e.dma_start(out=posi, in_=pos_d[s:s + 1, t0:t0 + T].partition_broadcast(128)),
                          "posi", writes=["posi"])
                    S.op("dve", lambda e: e.tensor_copy(qtmp[:, 1, :], posi), reads=["posi"], writes=["qtmp1"])
                    S.op("dve", lambda e: e.tensor_scalar(qtmp[:, 1, :], qtmp[:, 1, :], small[:, G_INVF:G_INVF + 1], None, ALU.mult),
                         reads=["qtmp1", "small"], writes=["qtmp1"])
                    sin_of(cosF[:], qtmp[:, 1, :], math.pi / 2, tmp[:, 0, :], tmp[:, 1, :], ["qtmp1"], ["cosF"], "tmp0", "tmp1")
                    sin_of(sinF[:], qtmp[:, 1, :], 0.0, tmp[:, 0, :], tmp[:, 1, :], ["qtmp1"], ["sinF"], "tmp0", "tmp1")
                    S.op("dve", lambda e: e.tensor_scalar(sinF[:], sinF[:], small[:, G_SGN:G_SGN + 1], None, ALU.mult),
                         reads=["sinF", "small"], writes=["sinF"])

                stopped = False
                for l in ([] if t1a else layers):
                  try:
                      chk("tile0")
                      S.dma("sp", lambda e, l=l, s=s, t0=t0: e.dma_start(out=ptok[:], in_=p_d[l, s, t0:t0 + T, :].rearrange("(tb p) d -> p tb d", p=128)),
                            "ptok", writes=["ptok"])
                      S.op("act", lambda e: e.copy(ptokb[:], ptok[:]), reads=["ptok"], writes=["ptokb"])
                      for c in range(2):
                          def trp(e, c=c):
                              ins = None
                              for tb in range(4):
                                  ins = e.transpose(psb[:, c, tb, :], ptokb[:, tb, c * 128:(c + 1) * 128], identb[:])
                              return ins
                          S.op("pe", trp, reads=["ptokb", "identb"], writes=["psb%d" % c])
                          S.op("dve", lambda e, c=c: e.tensor_copy(pT[:, c, :].rearrange("p (tb t) -> p tb t", tb=4), psb[:, c, 0:4, :]), reads=["psb%d" % c], writes=["pT"])
                      rmsnorm_to_hT(sm(l, "mixg"))
                      chk("p0")
                      if first:
                          S.op("pool", lambda e, l=l: e.memset(ucar[:, l, :, :], 0.0), writes=["ucar"])
                          S.op("pool", lambda e, l=l: e.memset(scar[:, l, :, :], 0.0), writes=["scar"])
                      S.op("pool", lambda e, l=l: e.tensor_copy(kT[:, :, 0:128], kcar[:, l, :, :]), reads=["kcar"], writes=["kT"])
                      S.op("pool", lambda e, l=l: e.tensor_copy(Vt[:, 0, :], vcar[:, l, :]), reads=["vcar"], writes=["Vt"])
                      S.op("pool", lambda e, l=l: e.tensor_copy(ucb[:, :, 0:30], ucar[:, l, :, :]), reads=["ucar"], writes=["ucb"])
                      S.dma("sp", lambda e, l=l: e.dma_start(out=bpad_t[:].rearrange("p a n -> p (a n)"),
                                                               in_=wscr[l, PIECES["bpad"][0][0]:PIECES["bpad"][0][0] + 128 * 2048].rearrange("(p f) -> p f", p=128)),
                            "bpad", reads=["wscr%d" % l], writes=["bpad"])
                      S.dma("sp", lambda e, l=l: e.dma_start(out=cpad_t[:].rearrange("p a n -> p (a n)"), in_=cscr[l, :, :]), "cpad",
                            reads=["cscr%d" % l], writes=["cpad"])
                      S.dma("sp", lambda e, l=l: e.dma_start(out=tabf, in_=tscr[l, :, :]), "tab", reads=["tscr%d" % l], writes=["tab"])
                      chk("p1")
                      win = PIECES["win"]
                      sinks = sm(l, "sinks")

                      def rope_pair(wA, wkA, iA, wB, wkB, iB, dst, dkey):
                          ps, pk = bank()
                          mm_chain(ps, [(wA[:, k, iA * 128:(iA + 1) * 128], hT[:, k, :]) for k in range(8)], [wkA, "hT"], pk)
                          S.op("dve", lambda e: e.tensor_tensor(qtmp[:, 0, :], ps, cosF[:], ALU.mult), reads=[pk, "cosF"], writes=["qtmp"])
                          ps2, pk2 = bank()
                          mm_chain(ps2, [(wB[:, k, iB * 128:(iB + 1) * 128], hT[:, k, :]) for k in range(8)], [wkB, "hT"], pk2)
                          S.op("dve", lambda e: e.tensor_tensor(qtmp[:, 1, :], ps2, sinF[:], ALU.mult), reads=[pk2, "sinF"], writes=["qtmp1"])
                          S.op("dve", lambda e: e.tensor_tensor(dst, qtmp[:, 0, :], qtmp[:, 1, :], ALU.add), reads=["qtmp", "qtmp1"], writes=[dkey])

                      wv0, wk0 = wload(l, win[0])
                      wv1, wk1 = wload(l, win[1])
                      for i in range(4):
                          rope_pair(wv0, wk0, i, wv1, wk1, i, qT[:, i, :], "qT")
                      chk("p2")
                      wv2, wk2 = wload(l, win[2])
                      for i in range(2):
                          rope_pair(wv2, wk2, i, wv2, wk2, i + 2, kT[:, i, 128:640], "kT")
                      chk("p3")
                      wv3, wk3 = wload(l, win[3])
                      for i in range(2):
                          ps, pk = bank()
                          mm_chain(ps, [(wv3[:, k, i * 128:(i + 1) * 128], hT[:, k, :]) for k in range(8)], [wk3, "hT"], pk)
                          S.op("act", lambda e: e.copy(uT32[:, i, :], ps), reads=[pk], writes=["uT32"])
                          S.op("dve", lambda e: e.tensor_copy(uTb[:, i, :], uT32[:, i, :]), reads=["uT32"], writes=["uTb"])
                      ps, pk = bank()

                      def vproj(e):
                          ins = None
                          for tb in range(4):
                              for k in range(8):
                                  ins = e.matmul(ps[:, tb * 128:(tb + 1) * 128], hT[:, k, tb * 128:(tb + 1) * 128], wv3[:, k, 256:384],
                                                 start=(k == 0), stop=(k == 7))
                          return ins
                      S.op("pe", vproj, reads=[wk3, "hT"], writes=[pk])
                      S.op("act", lambda e: e.copy(Vt[:, 1:5, :], ps.rearrange("p (tb d) -> p tb d", tb=4)), reads=[pk], writes=["Vt"])
                      chk("p4")
                      wv4, wk4 = wload(l, win[4])
                      for i in range(2):
                          psg, pkg = bank()
                          mm_chain(psg, [(wv4[:, k, (i + 2) * 128:(i + 3) * 128], hT[:, k, :]) for k in range(8)], [wk4, "hT"], pkg)
                          S.op("act", lambda e: e.activation(tmp[:, 2, :], psg, AF.Sigmoid), reads=[pkg], writes=["tmp2"])
                          psa, pka = bank()
                          mm_chain(psa, [(wv4[:, k, i * 128:(i + 1) * 128], hT[:, k, :]) for k in range(8)], [wk4, "hT"], pka)
                          S.op("dve", lambda e: e.tensor_tensor(ucb[:, i, 30:30 + T], psa, tmp[:, 2, :], ALU.mult), reads=[pka, "tmp2"], writes=["ucb"])
                      chk("p5")
                      for c in range(2):
                          wvd, wkd = wload(l, (0, 31, 128), src=dscr[l, c, :, :], srckey="dscr%d" % l)
                          ps, pk = bank()
                          mm_chain(ps, [(wvd[:, k, :], ucb[:, c, k:k + T]) for k in range(31)], [wkd, "ucb"], pk)
                          S.op("act", lambda e: e.activation(cacc[:, c, :], ps, AF.Identity, bias=sm(l, "convb", c, c + 1)), reads=[pk, "small"], writes=["cacc%d" % c])
                      S.op("pool", lambda e, l=l: e.tensor_copy(ucar[:, l, :, :], ucb[:, :, T:T + 30]), reads=["ucb"], writes=["ucar"])
                      chk("proj")
                      psm, pkm = bank()
                      mm_chain(psm, [(onesf[:], cacc[:, c, :]) for c in range(2)], ["onesf", "cacc0", "cacc1"], pkm)
                      for c in range(2):
                          S.op("act", lambda e: e.activation(tmp[:, c, :], cacc[:, c, :], AF.Square), reads=["cacc%d" % c], writes=["tmp%d" % c])
                      psq, pkq = bank()
                      mm_chain(psq, [(onesf[:], tmp[:, c, :]) for c in range(2)], ["onesf", "tmp0", "tmp1"], pkq)
                      S.op("act", lambda e: e.copy(tmp[:, 2, :], psm), reads=[pkm], writes=["tmp2"])
                      S.op("dve", lambda e: e.tensor_tensor(tmp[:, 3, :], tmp[:, 2, :], tmp[:, 2, :], ALU.mult), reads=["tmp2"], writes=["tmp3"])
                      S.op("dve", lambda e: e.tensor_tensor(tmp[:, 3, :], psq, tmp[:, 3, :], ALU.subtract), reads=[pkq, "tmp3"], writes=["tmp3"])
                      S.op("act", lambda e: e.activation(tmp[:, 3, :], tmp[:, 3, :], AF.Sqrt, bias=EPS), reads=["tmp3"], writes=["tmp3"])
                      S.op("dve", lambda e: e.reciprocal(tmp[:, 3, :], tmp[:, 3, :]), reads=["tmp3"], writes=["tmp3"])
                      for c in range(2):
                          S.op("dve", lambda e: e.tensor_tensor(cacc[:, c, :], cacc[:, c, :], tmp[:, 2, :], ALU.subtract),
                               reads=["cacc%d" % c, "tmp2"], writes=["cacc%d" % c])
                          S.op("dve", lambda e: e.tensor_tensor(cacc[:, c, :], cacc[:, c, :], tmp[:, 3, :], ALU.mult),
                               reads=["cacc%d" % c, "tmp3"], writes=["cacc%d" % c])
                          S.op("act", lambda e: e.activation(convact[:, c, :], cacc[:, c, :], AF.Silu, scale=sm(l, "lng", c, c + 1), bias=sm(l, "lnb", c, c + 1)),
                               reads=["cacc%d" % c, "small"], writes=["convact"])

                      v4 = lambda ap: ap.rearrange("p (s t) -> p s t", s=4)
                      psy = [None, None]

                      def ssm_front(bi):
                          cu = bi // 2
                          gp0 = bi * 2
                          X = w4 if bi % 2 == 0 else r4
                          xk = "w4" if bi % 2 == 0 else "r4"
                          for g2 in range(2):
                              gp = gp0 + g2
                              psr, pkr = bank()
                              mm_chain(psr, [(bpad_t[:, gp * 2 + 0, :], uTb[:, cu, :])], ["bpad", "uTb"], pkr)
                              psi, pki = bank()
                              mm_chain(psi, [(bpad_t[:, gp * 2 + 1, :], uTb[:, cu, :])], ["bpad", "uTb"], pki)
                              ct = tab[:, 0, gp, :].unsqueeze(1).to_broadcast([128, 4, 128])
                              stt = tab[:, 1, gp, :].unsqueeze(1).to_broadcast([128, 4, 128])
                              wre, wim = v4(X[:, 0, g2, :]), v4(X[:, 1, g2, :])
                              t0v, t1v = v4(tmp[:, 0, :]), v4(tmp[:, 1, :])
                              S.op("dve", lambda e: e.tensor_tensor(t0v, v4(psr), ct, ALU.mult), reads=[pkr, "tab"], writes=["tmp0"])
                              S.op("dve", lambda e: e.tensor_tensor(t1v, v4(psi), stt, ALU.mult), reads=[pki, "tab"], writes=["tmp1"])
                              S.op("dve", lambda e: e.tensor_tensor(wre, t0v, t1v, ALU.add), reads=["tmp0", "tmp1"], writes=[xk])
                              S.op("dve", lambda e: e.tensor_tensor(t0v, v4(psi), ct, ALU.mult), reads=[pki, "tab", xk], writes=["tmp0"])
                              S.op("dve", lambda e: e.tensor_tensor(t1v, v4(psr), stt, ALU.mult), reads=[pkr, "tab", xk], writes=["tmp1"])
                              S.op("dve", lambda e: e.tensor_tensor(wim, t0v, t1v, ALU.subtract), reads=["tmp0", "tmp1"], writes=[xk])
                          for sub in range(4):
                              c0, c1 = sub * 128, (sub + 1) * 128
                              for g2 in range(2):
                                  gp = gp0 + g2
                                  for ri in range(2):
                                      S.op("dve", lambda e: e.tensor_tensor_scan(
                                          X[:, ri, g2, c0:c1], rho[:, l, gp:gp + 1].to_broadcast([128, 128]), X[:, ri, g2, c0:c1],
                                          scar[:, l, ri, gp:gp + 1], ALU.mult, ALU.add),
                                          reads=[xk, "rho", "scar"], writes=[xk])
                              rre_e, rim_e = X[:, 0, :, c1 - 1], X[:, 1, :, c1 - 1]
                              ct_e, st_e = tab[:, 0, gp0:gp0 + 2, 127], tab[:, 1, gp0:gp0 + 2, 127]
                              A_, B_ = s8[:, 20, 0:2], s8[:, 21, 0:2]
                              S.op("dve", lambda e: e.tensor_tensor(A_, ct_e, rre_e, ALU.mult), reads=[xk, "tab"], writes=["s8a"])
                              S.op("dve", lambda e: e.tensor_tensor(B_, st_e, rim_e, ALU.mult), reads=[xk, "tab"], writes=["s8b"])
                              S.op("dve", lambda e: e.tensor_tensor(scar[:, l, 0, gp0:gp0 + 2], A_, B_, ALU.subtract), reads=["s8a", "s8b"], writes=["scar"])
                              S.op("dve", lambda e: e.tensor_tensor(A_, st_e, rre_e, ALU.mult), reads=[xk, "tab", "scar"], writes=["s8a"])
                              S.op("dve", lambda e: e.tensor_tensor(B_, ct_e, rim_e, ALU.mult), reads=[xk, "tab", "scar"], writes=["s8b"])
                              S.op("dve", lambda e: e.tensor_tensor(scar[:, l, 1, gp0:gp0 + 2], A_, B_, ALU.add), reads=["s8a", "s8b"], writes=["scar"])

                      def ssm_demod(bi):
                          gp0 = bi * 2
                          X = w4 if bi % 2 == 0 else r4
                          xk = "w4" if bi % 2 == 0 else "r4"
                          for g2 in range(2):
                              gp = gp0 + g2
                              u = gp % 2
                              ct = tab[:, 0, gp, :].unsqueeze(1).to_broadcast([128, 4, 128])
                              stt = tab[:, 1, gp, :].unsqueeze(1).to_broadcast([128, 4, 128])
                              rre, rim = v4(X[:, 0, g2, :]), v4(X[:, 1, g2, :])
                              for pi, (ta, rb) in enumerate(((ct, rre), (stt, rim), (stt, rre), (ct, rim))):
                                  eng = "pool"
                                  S.op(eng, lambda e: e.tensor_tensor(v4(Pp[:, u, pi, :]), rb, ta, ALU.mult), reads=[xk, "tab"], writes=["Pp%d" % u])

                      def ssm_back(bi):
                          cu = bi // 2
                          if bi % 2 == 0:
                              psy[cu] = bank(hold=True)
                          ps_, pk_ = psy[cu]
                          for g2 in range(2):
                              gp = bi * 2 + g2
                              u = gp % 2
                              first_mm = (gp % 4 == 0)
                              last_mm = (gp % 4 == 3)

                              def cmm(e):
                                  ins = None
                                  for pi, var in enumerate((0, 1, 2, 2)):
                                      ins = e.matmul(ps_, cpad_t[:, gp * 3 + var, :], Pp[:, u, pi, :],
                                                     start=(first_mm and pi == 0), stop=(last_mm and pi == 3))
                                  return ins
                              S.op("pe", cmm, reads=["Pp%d" % u, "cpad"], writes=[pk_])

                      def gates_group(g):
                          wvg, wkg = wload(l, win[5 + g])
                          for i in range(4):
                              ci = g * 4 + i
                              ps, pk = bank()
                              mm_chain(ps, [(wvg[:, k, i * 128:(i + 1) * 128], hT[:, k, :]) for k in range(8)], [wkg, "hT"], pk)
                              S.op("act", lambda e: e.activation(gates[:, ci, :], ps, AF.Sigmoid, bias=sm(l, "bgate", ci, ci + 1)),
                                   reads=[pk, "small"], writes=["gates"])

                      psos = [None] * 4
                      psk = [None, None]

                      def attn_stage_a(un):
                          qb, kv, hh, u = un
                          hps = (2 * kv, 2 * kv + 1)
                          pss, pks = bank()
                          psk[u] = (pss, pks)
                          mi = 1 if (first and qb == 0) else 0

                          def scores(e):
                              ins = e.matmul(pss, identb[:], maskb[:, mi, :, :].rearrange("p h k -> p (h k)"), start=True, stop=False)
                              for i, hp in enumerate(hps):
                                  ins = e.matmul(pss[:, i * 256:(i + 1) * 256],
                                                 qT[hh * 64:(hh + 1) * 64, hp, qb * 128:(qb + 1) * 128],
                                                 kT[hh * 64:(hh + 1) * 64, kv, qb * 128:qb * 128 + 256], start=False, stop=(i == 1))
                              return ins
                          S.op("pe", scores, reads=["qT", "kT", "identb", "maskb"], writes=[pks])
                          stk = "stat%d" % u
                          h0 = 4 * kv + hh
                          snk = sinks[:, h0:h0 + 3:2]
                          S.op("dve", lambda e: e.tensor_reduce(stat[:, u, 0:2], pss.rearrange("p (h k) -> p h k", h=2), AX.X, ALU.max), reads=[pks], writes=[stk])
                          S.op("dve", lambda e: e.scalar_tensor_tensor(stat[:, u, 2:4], stat[:, u, 0:2], 0.125, snk, ALU.mult, ALU.max),
                               reads=[stk, "small"], writes=[stk])
                          S.op("dve", lambda e: e.tensor_scalar(stat[:, u, 4:6], stat[:, u, 2:4], -1.0, None, ALU.mult), reads=[stk], writes=[stk])
                          S.op("dve", lambda e: e.tensor_tensor(stat[:, u, 6:8], snk, stat[:, u, 2:4], ALU.subtract), reads=[stk, "small"], writes=[stk])

                      def attn_stage_b(un):
                          qb, kv, hh, u = un
                          pbk, stk = "Pb%d" % u, "stat%d" % u
                          pss, pks = psk[u]
                          for i in range(2):
                              S.op("act", lambda e: e.activation(Pb[:, u, i, :], pss[:, i * 256:(i + 1) * 256], AF.Exp, scale=0.125,
                                                                 bias=stat[:, u, 4 + i:5 + i], accum_out=stat[:, u, 8 + i:9 + i]),
                                   reads=[pks, stk], writes=[pbk, stk])
                          S.op("act", lambda e: e.activation(stat[:, u, 10:12], stat[:, u, 6:8], AF.Exp), reads=[stk], writes=[stk])
                          S.op("dve", lambda e: e.tensor_tensor(stat[:, u, 12:14], stat[:, u, 8:10], stat[:, u, 10:12], ALU.add), reads=[stk], writes=[stk])
                          S.op("act", lambda e: e.activation(stat[:, u, 14:16], stat[:, u, 12:14], AF.Ln), reads=[stk], writes=[stk])
                          S.op("dve", lambda e: e.tensor_tensor(stat[:, u, 14:16], stat[:, u, 4:6], stat[:, u, 14:16], ALU.subtract), reads=[stk], writes=[stk])
                          for i in range(2):
                              S.op("act", lambda e: e.activation(Pb[:, u, i, :], pss[:, i * 256:(i + 1) * 256], AF.Exp, scale=0.125,
                                                                 bias=stat[:, u, 14 + i:15 + i]),
                                   reads=[pks, stk], writes=[pbk])

                      def attn_stage_c(un):
                          qb, kv, hh, u = un
                          hps = (2 * kv, 2 * kv + 1)
                          pbk, ptk = "Pb%d" % u, "PTs%d" % u

                          def ptrans(e):
                              ins = None
                              for i in range(2):
                                  for kb in range(2):
                                      ins = e.transpose(psb[:, u, i * 2 + kb, :], Pb[:, u, i, kb * 128:(kb + 1) * 128], identb[:])
                              return ins
                          S.op("pe", ptrans, reads=[pbk, "identb"], writes=["psb%d" % u])
                          S.op("act", lambda e: e.copy(PTs[:, u, :, :], psb[:, u, 0:4, :]), reads=["psb%d" % u], writes=[ptk])
                          for i, hp in enumerate(hps):
                              pso, pko = psos[hp]

                              def pv(e):
                                  ins = None
                                  for kb in range(2):
                                      ins = e.matmul(pso[hh * 64:(hh + 1) * 64, qb * 128:(qb + 1) * 128],
                                                     Vt[:, qb + kb, kv * 64:(kv + 1) * 64], PTs[:, u, i * 2 + kb, :],
                                                     start=(kb == 0), stop=(kb == 1))
                                  return ins
                              S.op("pe", pv, reads=[ptk, "Vt"], writes=[pko])

                      def attn_round(units):
                          order = [("a", 0), ("a", 1), ("b", 0), ("a", 2), ("b", 1), ("c", 0), ("a", 3), ("b", 2), ("c", 1), ("b", 3), ("c", 2), ("c", 3)]
                          fns = {"a": attn_stage_a, "b": attn_stage_b, "c": attn_stage_c}
                          for st_, n in order:
                              fns[st_](units[n])

                      def ssm_tail(cu):
                          ps_, pk_ = psy[cu]
                          S.op("dve", lambda e: e.scalar_tensor_tensor(ys_t[:, cu, :], uT32[:, cu, :], sm(l, "ssmd", cu, cu + 1), ps_, ALU.mult, ALU.add),
                               reads=[pk_, "uT32", "small"], writes=["cacc%d" % cu])
                          S.op("act", lambda e: e.activation(gT[:, cu, :], ys_t[:, cu, :], AF.Gelu_apprx_tanh), reads=["cacc%d" % cu], writes=["convact2"])
                          release(pk_)

                      gsched = ((0, 1), (2,), (3, 4), (5,))
                      for r in range(4):
                          kv = r // 2
                          if r % 2 == 0:
                              psos[2 * kv] = bank(hold=True)
                              psos[2 * kv + 1] = bank(hold=True)
                          ssm_front(r)
                          for g in gsched[r]:
                              gates_group(g)
                          units = []
                          for qb in (2 * (r % 2), 2 * (r % 2) + 1):
                              for hh in range(2):
                                  units.append((qb, kv, hh, len(units) % 2))
                          attn_round(units)
                          if r % 2 == 1:
                              for hp in (2 * kv, 2 * kv + 1):
                                  pso, pko = psos[hp]
                                  S.op("act", lambda e: e.copy(yattn[:, hp, :], pso), reads=[pko], writes=["yattn"])
                                  release(pko)
                          if r >= 1:
                              ssm_back(r - 1)
                              if (r - 1) % 2 == 1:
                                  ssm_tail((r - 1) // 2)
                          ssm_demod(r)
                      chk("attn")
                      S.op("pool", lambda e, l=l: e.tensor_copy(kcar[:, l, :, :], kT[:, :, 512:640]), reads=["kT"], writes=["kcar"])
                      S.op("pool", lambda e, l=l: e.tensor_copy(vcar[:, l, :], Vt[:, 4, :]), reads=["Vt"], writes=["vcar"])
                      for g in range(2):
                          wva, wka = wload(l, PIECES["wao"][g])
                          for i in range(4):
                              m = g * 4 + i
                              ps, pk = bank()
                              mm_chain(ps, [(wva[:, k, i * 128:(i + 1) * 128], yattn[:, k, :]) for k in range(4)], [wka, "yattn"], pk)
                              S.op("dve", lambda e: e.tensor_tensor(merged[:, m, :], ps, gates[:, m, :], ALU.mult), reads=[pk, "gates"], writes=["merged"])
                      for g in range(2):
                          wvc, wkc = wload(l, PIECES["wco"][g])
                          for i in range(4):
                              m = g * 4 + i
                              ps, pk = bank()
                              mm_chain(ps, [(wvc[:, k, i * 128:(i + 1) * 128], convact[:, k, :]) for k in range(2)], [wkc, "convact"], pk)
                              S.op("dve", lambda e: e.tensor_tensor(tmp[:, 0, :], ps, gates[:, 16 + m, :], ALU.mult), reads=[pk, "gates"], writes=["tmp0"])
                              S.op("dve", lambda e: e.tensor_tensor(merged[:, m, :], merged[:, m, :], tmp[:, 0, :], ALU.add), reads=["tmp0", "merged"], writes=["merged"])
                      chk("conv")
                      ssm_back(3)
                      ssm_tail(1)
                      for g in range(4):
                          wvg, wkg = wload(l, PIECES["wglu"][g])
                          for i in range(2):
                              m = g * 2 + i
                              psb_, pkb_ = bank()
                              mm_chain(psb_, [(wvg[:, k, (i + 2) * 128:(i + 3) * 128], gT[:, k, :]) for k in range(2)], [wkg, "convact2"], pkb_)
                              ta_, tb_ = 2 + (m % 2), m % 2
                              S.op("act", lambda e: e.activation(tmp[:, ta_, :], psb_, AF.Sigmoid, bias=sm(l, "bglu", 8 + m, 9 + m)), reads=[pkb_, "small"], writes=["tmp%d" % ta_])
                              psa_, pka_ = bank()
                              mm_chain(psa_, [(wvg[:, k, i * 128:(i + 1) * 128], gT[:, k, :]) for k in range(2)], [wkg, "convact2"], pka_)
                              S.op("dve", lambda e: e.scalar_tensor_tensor(tmp[:, tb_, :], psa_, sm(l, "bglu", m, m + 1), tmp[:, ta_, :], ALU.add, ALU.mult),
                                   reads=[pka_, "tmp%d" % ta_, "small"], writes=["tmp%d" % tb_])
                              S.op("dve", lambda e: e.tensor_tensor(tmp[:, tb_, :], tmp[:, tb_, :], gates[:, 8 + m, :], ALU.mult), reads=["tmp%d" % tb_, "gates"], writes=["tmp%d" % tb_])
                              S.op("dve", lambda e: e.tensor_tensor(merged[:, m, :], merged[:, m, :], tmp[:, tb_, :], ALU.add), reads=["tmp%d" % tb_, "merged"], writes=["merged"])
                      chk("ssm")
                      for c in range(8):
                          S.op("act", lambda e, c=c: e.copy(hT[:, c, :], merged[:, c, :]), reads=["merged"], writes=["hT"])
                      for g in range(2):
                          wvm, wkm = wload(l, PIECES["wmo"][g])
                          for i in range(4):
                              m = g * 4 + i
                              ps, pk = bank()
                              mm_chain(ps, [(wvm[:, k, i * 128:(i + 1) * 128], hT[:, k, :]) for k in range(8)], [wkm, "hT"], pk)
                              S.op("dve", lambda e, m=m, ps=ps: e.tensor_tensor(xT[:, m, :], ps, xT[:, m, :], ALU.add), reads=[pk, "xT"], writes=["xT"])
                              sq_chunk(m)

                      chk("mix")
                      rmsnorm_to_hT(sm(l, "ffng"))
                      for jg in range(11):
                          wvf, wkf = wload(l, PIECES["wfi"][jg])
                          for i in range(2):
                              hc = jg * 2 + i
                              psg, pkg = bank()
                              mm_chain(psg, [(wvf[:, k, i * 128:(i + 1) * 128], hT[:, k, :]) for k in range(8)], [wkf, "hT"], pkg)
                              ta_ = 2 + (hc % 2)
                              S.op("act", lambda e, psg=psg: e.activation(tmp[:, ta_, :], psg, AF.Silu), reads=[pkg], writes=["tmp%d" % ta_])
                              psu, pku = bank()
                              mm_chain(psu, [(wvf[:, k, (i + 2) * 128:(i + 3) * 128], hT[:, k, :]) for k in range(8)], [wkf, "hT"], pku)
                              S.op("dve", lambda e, hc=hc, psu=psu: e.tensor_tensor(act_ffn[:, hc, :], psu, tmp[:, ta_, :], ALU.mult),
                                   reads=[pku, "tmp%d" % ta_], writes=["gates"])
                      for ng in range(2):
                          pss_ = [bank(hold=True) for _ in range(4)]
                          k0 = 0
                          for pi_, piece in enumerate(PIECES["wfo"][ng * 3:(ng + 1) * 3]):
                              wvo, wko = wload(l, piece)
                              kcp = piece[1]
                              for i in range(4):
                                  ps, pk = pss_[i]

                                  def fo(e, ps=ps, wvo=wvo, kcp=kcp, k0=k0, i=i):
                                      ins = None
                                      for k in range(kcp):
                                          ins = e.matmul(ps, wvo[:, k, i * 128:(i + 1) * 128], act_ffn[:, k0 + k, :],
                                                         start=(k0 + k == 0), stop=(k0 + k == 21))
                                      return ins
                                  S.op("pe", fo, reads=[wko, "gates"], writes=[pk])
                              k0 += kcp
                          for i in range(4):
                              m = ng * 4 + i
                              ps, pk = pss_[i]
                              S.op("dve", lambda e, m=m, ps=ps: e.tensor_tensor(xT[:, m, :], ps, xT[:, m, :], ALU.add), reads=[pk, "xT"], writes=["xT"])
                              sq_chunk(m)
                              release(pk)

                      chk("ffn")
                      dump("pT", pT[:].rearrange("p c t -> p (c t)"), "pT", 1024, (s, j, l))
                      dump("ptok", ptok[:].rearrange("p a d -> p (a d)"), "ptok", 1024, (s, j, l))
                      rmsnorm_to_hT(sm(l, "pleg"))
                      for g in range(2):
                          wvi, wki = wload(l, PIECES["wpi"][g])
                          wvp, wkp = wload(l, PIECES["wpg"][g])
                          for i in range(4):
                              m = g * 4 + i
                              psg, pkg = bank()
                              mm_chain(psg, [(wvp[:, k, i * 128:(i + 1) * 128], hT[:, k, :]) for k in range(8)], [wkp, "hT"], pkg)
                              ta_, tb_ = 2 + (m % 2), m % 2
                              S.op("act", lambda e, psg=psg: e.activation(tmp[:, ta_, :], psg, AF.Sigmoid), reads=[pkg], writes=["tmp%d" % ta_])
                              pse, pke = bank()
                              mm_chain(pse, [(wvi[:, k, i * 128:(i + 1) * 128], pT[:, k, :]) for k in range(2)], [wki, "pT"], pke)
                              S.op("dve", lambda e, pse=pse: e.tensor_tensor(tmp[:, tb_, :], pse, tmp[:, ta_, :], ALU.mult), reads=[pke, "tmp%d" % ta_], writes=["tmp%d" % tb_])
                              S.op("dve", lambda e, m=m: e.tensor_tensor(xT[:, m, :], xT[:, m, :], tmp[:, tb_, :], ALU.add), reads=["tmp%d" % tb_, "xT"], writes=["xT"])
                              sq_chunk(m)

                      chk("ple_end")
                  except _Stop:
                    stopped = True
                    break
                dump("xT", xT[:].rearrange("p c t -> p (c t)"), "xT", 4096, (s, j))
                if final_norm:
                    rmsnorm_to_hT(small[:, G_FINALG:G_FINALG + 8], out_f32=merged)
                    src, srck = merged, "merged"
                else:
                    src, srck = xT, "xT"
                ost = [w4[:].rearrange("p a b t -> p (a b t)"), r4[:].rearrange("p a b t -> p (a b t)")]
                for tb in range(4):
                    stg = ost[tb % 2]
                    stk_ = "w4" if tb % 2 == 0 else "r4"
                    for half in range(2):
                        ps, pk = bank()

                        def tro(e, ps=ps, tb=tb, half=half, src=src):
                            ins = None
                            for cc in range(4):
                                c = half * 4 + cc
                                ins = e.transpose(ps[:, cc * 128:(cc + 1) * 128], src[:, c, tb * 128:(tb + 1) * 128], identf)
                            return ins
                        S.op("pe", tro, reads=[srck, "cst"], writes=[pk])
                        copy_op(evac_engine(), stg[:, half * 512:(half + 1) * 512], ps, [pk], [stk_])
                    S.dma("sp", lambda e, s=s, t0=t0, tb=tb, stg=stg: e.dma_start(out=out_d[s, t0 + tb * 128:t0 + (tb + 1) * 128, :], in_=stg[:, 0:1024]),
                          "out", reads=[stk_], writes=["out"])
                if stopped:
                    break
            if stopped:
                break
        S.final_wait("sp", ["out"] + ["dbg_" + n for n in DBG])
        S.emit(st)
    return nc


_CACHE = {}


def kernel(**inputs):
    x = np.ascontiguousarray(np.asarray(inputs["x"], np.float32))
    p = np.asarray(inputs["p"], np.float32)
    pos = np.ascontiguousarray(np.asarray(inputs["positions"], np.int32))
    wf, craw, small, cst = host_prep(inputs)
    nb = x.shape[0] // NCORE
    nc = build(nb, x.shape[1] // T, list(range(DEPTH)))
    in_maps = []
    for c in range(NCORE):
        sl = slice(c * nb, (c + 1) * nb)
        in_maps.append({"x": x[sl], "p": np.ascontiguousarray(p[:, sl]), "pos": pos[sl], "wf": wf, "craw": craw,
                        "small": small, "cst": cst})
    res = run_bass_kernel_spmd(nc, in_maps, core_ids=list(range(NCORE)))
    return np.concatenate([r["out"] for r in res.results], axis=0).astype(np.float32)
```
